# Optimizing a Trainium2 kernel written in Bass

```python
import math
import jax, jax.numpy as jnp
from jax import lax
import numpy as np

D_MODEL = 1024
BATCH = 8
SEQ = 4096
DEPTH = 2

CHUNK = 64
BRANCH_WIDTH = 512
N_BRANCH = 4
CONV_WIDTH = 3
SG_CHUNK = 128
SG_GROUPS = 4
SG_GROUP_DIM = BRANCH_WIDTH // SG_GROUPS
ATT_HEADS = 4
ATT_QK_DIM = 64
ATT_V_DIM = 2 * ATT_QK_DIM
Q_BLOCK = 128
SSM_GROUP = 16
SSM_GROUPS = BRANCH_WIDTH // SSM_GROUP
SSM_STATE = 64
FFN_HIDDEN = -(-8 * D_MODEL // (3 * 256)) * 256

QK_COLS = ATT_HEADS * 2 * ATT_QK_DIM
V_COLS = ATT_HEADS * ATT_V_DIM
SPLIT_SIZES = [BRANCH_WIDTH, BRANCH_WIDTH, BRANCH_WIDTH,
               2 * BRANCH_WIDTH,
               QK_COLS, QK_COLS, V_COLS,
               BRANCH_WIDTH,
               N_BRANCH * D_MODEL]
IN_COLS = sum(SPLIT_SIZES)
SPLIT_POINTS = [int(p) for p in np.cumsum(SPLIT_SIZES)[:-1]]

kernel_name = 'hybrid_gated_parallel_streaming_encoder'


def rmsnorm(x, g, eps=1e-6):
    xf = x.astype(jnp.float32)
    y = xf * lax.rsqrt(jnp.mean(xf * xf, axis=-1, keepdims=True) + eps)
    return (y * g.astype(jnp.float32)).astype(x.dtype)


def layernorm(x, g, b, eps=1e-5):
    xf = x.astype(jnp.float32)
    mu = jnp.mean(xf, axis=-1, keepdims=True)
    var = jnp.mean(jnp.square(xf - mu), axis=-1, keepdims=True)
    y = (xf - mu) * lax.rsqrt(var + eps)
    return (y * g.astype(jnp.float32) + b.astype(jnp.float32)).astype(x.dtype)


def short_conv_mixer(b_gate, c_gate, xin, conv_w, conv_b):
    z = c_gate * xin
    z = lax.conv_general_dilated(
        z, conv_w[:, None, :], window_strides=(1,), padding=[(CONV_WIDTH - 1, 0)],
        dimension_numbers=('NWC', 'WIO', 'NWC'), feature_group_count=BRANCH_WIDTH)
    return b_gate * (z + conv_b)


def spatial_gating_mixer(z, sg_w, sg_b, ln_g, ln_b):
    z = jax.nn.gelu(z)
    u, v = jnp.split(z, 2, axis=-1)
    v = layernorm(v, ln_g, ln_b)
    bsz, s, _ = v.shape
    v = v.reshape(bsz, s // SG_CHUNK, SG_CHUNK, SG_GROUPS, SG_GROUP_DIM)
    tri = jnp.tril(jnp.ones((SG_CHUNK, SG_CHUNK), dtype=bool))
    w = jnp.where(tri, sg_w, jnp.zeros_like(sg_w))
    mixed = jnp.einsum('gts,bcsgd->bctgd', w, v) + sg_b.T[:, :, None]
    return u * mixed.reshape(bsz, s, BRANCH_WIDTH)


def diff_attention_mixer(q, k, v, lam_qk, subln_g, lam_init):
    bsz, s, _ = q.shape
    q = q.reshape(bsz, s, ATT_HEADS, 2, ATT_QK_DIM)
    k = k.reshape(bsz, s, ATT_HEADS, 2, ATT_QK_DIM)
    v = v.reshape(bsz, s, ATT_HEADS, ATT_V_DIM)
    lf = lam_qk.astype(jnp.float32)
    lam = jnp.exp(jnp.sum(lf[0] * lf[1])) - jnp.exp(jnp.sum(lf[2] * lf[3])) + lam_init
    scale = ATT_QK_DIM ** -0.5
    n_blocks = s // Q_BLOCK
    q_blocks = q.reshape(bsz, n_blocks, Q_BLOCK, ATT_HEADS, 2, ATT_QK_DIM).transpose(1, 0, 2, 3, 4, 5)
    key_chunk = jnp.arange(s) // CHUNK

    def attend(args):
        qb, blk = args
        q_chunk = (blk * Q_BLOCK + jnp.arange(Q_BLOCK)) // CHUNK
        allowed = key_chunk[None, :] <= q_chunk[:, None]
        sc = jnp.einsum('bqhmd,bkhmd->bhmqk', qb, k).astype(jnp.float32) * scale
        sc = jnp.where(allowed, sc, -jnp.inf)
        p = jax.nn.softmax(sc, axis=-1)
        attn = p[:, :, 0] - lam * p[:, :, 1]
        return jnp.einsum('bhqk,bkhd->bqhd', attn.astype(v.dtype), v)

    o = lax.map(attend, (q_blocks, jnp.arange(n_blocks)))
    o = o.transpose(1, 0, 2, 3, 4).reshape(bsz, s, ATT_HEADS, ATT_V_DIM)
    o = rmsnorm(o, subln_g, eps=1e-5) * (1.0 - lam_init)
    return o.reshape(bsz, s, BRANCH_WIDTH)


def _linear_recurrence_combine(earlier, later):
    ar1, ai1, br1, bi1 = earlier
    ar2, ai2, br2, bi2 = later
    return (ar2 * ar1 - ai2 * ai1,
            ar2 * ai1 + ai2 * ar1,
            ar2 * br1 - ai2 * bi1 + br2,
            ar2 * bi1 + ai2 * br1 + bi2)


def s5_mixer(u, a_re, a_im, log_dt, b_re, b_im, c_re, c_im, d_skip, w_glu, b_glu):
    dtype = u.dtype
    f32 = jnp.float32
    bsz, s, _ = u.shape
    uf = u.astype(f32).reshape(bsz, s, SSM_GROUPS, SSM_GROUP)
    a_re, a_im = a_re.astype(f32), a_im.astype(f32)
    b_re, b_im = b_re.astype(f32), b_im.astype(f32)
    c_re, c_im = c_re.astype(f32), c_im.astype(f32)
    dt = jnp.exp(log_dt.astype(f32))[:, None]
    mag = jnp.exp(dt * a_re)
    ab_re = mag * jnp.cos(dt * a_im)
    ab_im = mag * jnp.sin(dt * a_im)
    den = a_re * a_re + a_im * a_im
    nr, ni = ab_re - 1.0, ab_im
    coef_re = (nr * a_re + ni * a_im) / den
    coef_im = (ni * a_re - nr * a_im) / den
    bb_re = coef_re[..., None] * b_re - coef_im[..., None] * b_im
    bb_im = coef_re[..., None] * b_im + coef_im[..., None] * b_re
    bu_re = jnp.einsum('bsgh,gph->bsgp', uf, bb_re)
    bu_im = jnp.einsum('bsgh,gph->bsgp', uf, bb_im)
    a_seq_re = jnp.broadcast_to(ab_re, bu_re.shape)
    a_seq_im = jnp.broadcast_to(ab_im, bu_re.shape)
    _, _, x_re, x_im = lax.associative_scan(
        _linear_recurrence_combine, (a_seq_re, a_seq_im, bu_re, bu_im), axis=1)
    y = jnp.einsum('bsgp,ghp->bsgh', x_re, c_re) - jnp.einsum('bsgp,ghp->bsgh', x_im, c_im)
    y = y.reshape(bsz, s, BRANCH_WIDTH) + d_skip.astype(f32) * uf.reshape(bsz, s, BRANCH_WIDTH)
    y = jax.nn.gelu(y)
    y = y * jax.nn.sigmoid(y @ w_glu.astype(f32) + b_glu.astype(f32))
    return y.astype(dtype)


def setup_inputs(seed: int = 0) -> dict:
    key = jax.random.key(seed)
    ks = jax.random.split(key, 32)
    f32 = jnp.float32
    L, D, W = DEPTH, D_MODEL, BRANCH_WIDTH
    G, P, H = SSM_GROUPS, SSM_STATE, SSM_GROUP

    def nrm(k, shape, scale):
        return jax.random.normal(k, shape, f32) * scale

    n_idx = jnp.arange(P, dtype=f32)
    return {
        'x': nrm(ks[0], (BATCH, SEQ, D), 1.0),
        'g_mix': 1.0 + nrm(ks[1], (L, D), 0.02),
        'w_in': nrm(ks[2], (L, D, IN_COLS), D ** -0.5),
        'conv_w': nrm(ks[3], (L, CONV_WIDTH, W), CONV_WIDTH ** -0.5),
        'conv_b': nrm(ks[4], (L, W), 0.02),
        'sg_w': nrm(ks[5], (L, SG_GROUPS, SG_CHUNK, SG_CHUNK), SG_CHUNK ** -0.5),
        'sg_b': 1.0 + nrm(ks[6], (L, SG_GROUPS, SG_CHUNK), 0.02),
        'sg_ln_g': 1.0 + nrm(ks[7], (L, W), 0.02),
        'sg_ln_b': nrm(ks[8], (L, W), 0.02),
        'lam_qk': nrm(ks[9], (L, 4, ATT_QK_DIM), 0.1),
        'subln_g': 1.0 + nrm(ks[10], (L, ATT_V_DIM), 0.02),
        'ssm_a_re': -0.5 + nrm(ks[11], (L, G, P), 0.01),
        'ssm_a_im': jnp.pi * n_idx + nrm(ks[12], (L, G, P), 0.01),
        'ssm_log_dt': jax.random.uniform(ks[13], (L, G), f32, math.log(1e-3), math.log(1e-1)),
        'ssm_b_re': nrm(ks[14], (L, G, P, H), (2 * H) ** -0.5),
        'ssm_b_im': nrm(ks[15], (L, G, P, H), (2 * H) ** -0.5),
        'ssm_c_re': nrm(ks[16], (L, G, H, P), P ** -0.5),
        'ssm_c_im': nrm(ks[17], (L, G, H, P), P ** -0.5),
        'ssm_d': nrm(ks[18], (L, W), 0.5),
        'w_glu': nrm(ks[19], (L, W, W), W ** -0.5),
        'b_glu': nrm(ks[20], (L, W), 0.02),
        'w_br': nrm(ks[21], (L, N_BRANCH, W, D), W ** -0.5),
        'w_o': nrm(ks[22], (L, D, D), D ** -0.5),
        'g_ffn': 1.0 + nrm(ks[23], (L, D), 0.02),
        'w_ffn_gate': nrm(ks[24], (L, D, FFN_HIDDEN), D ** -0.5),
        'w_ffn_up': nrm(ks[25], (L, D, FFN_HIDDEN), D ** -0.5),
        'w_ffn_down': nrm(ks[26], (L, FFN_HIDDEN, D), FFN_HIDDEN ** -0.5),
        'g_final': 1.0 + nrm(ks[27], (D,), 0.02),
    }


def reference(x, g_mix, w_in, conv_w, conv_b, sg_w, sg_b, sg_ln_g, sg_ln_b, lam_qk, subln_g,
              ssm_a_re, ssm_a_im, ssm_log_dt, ssm_b_re, ssm_b_im, ssm_c_re, ssm_c_im, ssm_d,
              w_glu, b_glu, w_br, w_o, g_ffn, w_ffn_gate, w_ffn_up, w_ffn_down, g_final):
    bsz, s, _ = x.shape
    for l in range(DEPTH):
        lam_init = 0.8 - 0.6 * math.exp(-0.3 * l)
        h = rmsnorm(x, g_mix[l])
        proj = h @ w_in[l]
        a_b, a_c, a_x, b_uv, c_q, c_k, c_v, d_u, gate_logits = jnp.split(proj, SPLIT_POINTS, axis=-1)

        y_a = short_conv_mixer(a_b, a_c, a_x, conv_w[l], conv_b[l])
        y_b = spatial_gating_mixer(b_uv, sg_w[l], sg_b[l], sg_ln_g[l], sg_ln_b[l])
        y_c = diff_attention_mixer(c_q, c_k, c_v, lam_qk[l], subln_g[l], lam_init)
        y_d = s5_mixer(d_u, ssm_a_re[l], ssm_a_im[l], ssm_log_dt[l], ssm_b_re[l], ssm_b_im[l],
                       ssm_c_re[l], ssm_c_im[l], ssm_d[l], w_glu[l], b_glu[l])

        gates = jax.nn.sigmoid(gate_logits).reshape(bsz, s, N_BRANCH, D_MODEL)
        branches = (y_a, y_b, y_c, y_d)
        merged = gates[:, :, 0] * (branches[0] @ w_br[l, 0])
        for n in range(1, N_BRANCH):
            merged = merged + gates[:, :, n] * (branches[n] @ w_br[l, n])
        x = x + merged @ w_o[l]

        h = rmsnorm(x, g_ffn[l])
        x = x + (jax.nn.silu(h @ w_ffn_gate[l]) * (h @ w_ffn_up[l])) @ w_ffn_down[l]
    return rmsnorm(x, g_final)
```

```python
import math
from contextlib import ExitStack

import numpy as np
import concourse.bass as bass
import concourse.mybir as mybir
from concourse.bass_utils import run_bass_kernel_spmd

F32 = mybir.dt.float32
BF16 = mybir.dt.bfloat16
AF = mybir.ActivationFunctionType
ALU = mybir.AluOpType
AX = mybir.AxisListType

TT = 512
SEQ = 4096
DM = 1024
NSLOT = 4
SLOT = 4096
NSS = 3
MAGIC = 12582912.0
PI = math.pi

REC = []
REC.append(("Du", 4096))
for _j in range(4):
    REC.append(("A%d" % _j, 3072))
for _n in ("Bu", "Bv", "Cq", "Ck", "Cv"):
    REC.append((_n, 4096))
REC.append(("Wglu", 2048))
for _m in range(8):
    REC.append(("G%d" % _m, 4096))
    REC.append(("BR%d" % _m, 2048))
REC.append(("WO0", 4096))
REC.append(("WO1", 4096))
for _f in range(11):
    REC.append(("F%d" % _f, 4096))
for _m in range(8):
    REC.append(("D%d" % _m, 2816))
REC_OFF = {}
_o = 0
for _n, _s in REC:
    REC_OFF[_n] = (_o, _s)
    _o += _s
NW = _o

CL = 48
C_GMIX, C_GFFN, C_CONVW, C_CONVB, C_BGLU, C_SSMD, C_SUBLN = 0, 8, 16, 28, 32, 36, 40
C_GFINAL = 96
C_IDENT = 104
C_JMAT = 232
NCST = 360
PR_L = 768
PR_TRIL = 1536
PR_BD = 1664
PR_M8 = 1792
NPR = 1800
SN_ARE, SN_AIM, SN_LDT, SN_BRE, SN_BIM, SN_CRE, SN_CIM = 0, 32, 64, 96, 608, 1120, 1632
ST_ARE, ST_AIM, ST_LDT, ST_BRE, ST_BIM = 2144, 2656, 3168, 3680, 4192
NSR = 4704


class Eng:
    def __init__(self, name, sem):
        self.name = name
        self.sem = sem
        self.cnt = 0
        self.ops = []
        self.seen = {}


class DmaSem:
    def __init__(self, sem):
        self.sem = sem
        self.cnt = 0


class Buf:
    __slots__ = ("name", "w", "r")

    def __init__(self, name=""):
        self.name = name
        self.w = None
        self.r = {}


class Pool_:
    def __init__(self, items):
        self.free = list(items)

    def get(self):
        assert self.free, "pool exhausted"
        return self.free.pop(0)

    def put(self, it):
        self.free.append(it)


class Prog:
    def __init__(self, nc, stack):
        self.nc = nc
        self.stack = stack
        self.eng = {}
        for n in ("pe", "act", "dve", "pool", "sp"):
            sem = stack.enter_context(nc.semaphore("sem_" + n))
            self.eng[n] = Eng(n, sem)
        self.pe_sems = {id(self.eng["pe"].sem)}

    def dma_sem(self, name):
        return DmaSem(self.stack.enter_context(self.nc.semaphore(name)))

    def new_epoch(self, tag):
        for n, E in self.eng.items():
            E.sem = self.stack.enter_context(self.nc.semaphore("sem_%s_%s" % (n, tag)))
            E.cnt = 0
            if n == "pe":
                self.pe_sems.add(id(E.sem))

    def op(self, eng, fn, reads=(), writes=(), inc=True, dma=None):
        E = self.eng[eng]
        waits = {}

        def need(c):
            if c is None:
                return
            sem, val = c
            if E.name == "pe" and id(sem) in self.pe_sems:
                return
            k = id(sem)
            if E.seen.get(k, 0) >= val:
                return
            if k not in waits or waits[k][1] < val:
                waits[k] = (sem, val)

        for b in reads:
            need(b.w)
        for b in writes:
            need(b.w)
            for c in b.r.values():
                need(c)
        for k, (sem, val) in waits.items():
            E.seen[k] = val
        if dma is not None:
            dma.cnt += 16
            comp = (dma.sem, dma.cnt)
            incspec = (dma.sem, 16)
        elif inc:
            E.cnt += 1
            comp = (E.sem, E.cnt)
            incspec = (E.sem, 1)
        else:
            comp = (E.sem, E.cnt + 1)
            incspec = None
        ck = id(comp[0])
        for b in reads:
            if ck not in b.r or b.r[ck][1] < comp[1]:
                b.r[ck] = comp
        for b in writes:
            b.w = comp
            b.r = {}
        E.ops.append((list(waits.values()), fn, incspec))
        return comp

    def wait_all(self, eng, comps):
        E = self.eng[eng]
        ws = []
        for sem, val in comps:
            if E.seen.get(id(sem), 0) < val:
                E.seen[id(sem)] = val
                ws.append((sem, val))
        E.ops.append((ws, None, None))

    def barrier(self, dma_sems=()):
        comps = [(E.sem, E.cnt) for E in self.eng.values() if E.cnt > 0]
        comps += [(d.sem, d.cnt) for d in dma_sems if d.cnt > 0]
        for n in self.eng:
            self.wait_all(n, comps)

    def replay(self):
        nc = self.nc
        with nc.Block() as block:
            def mk(E):
                ops = E.ops

                def body(e):
                    for waits, fn, incspec in ops:
                        for sem, val in waits:
                            e.wait_ge(sem, val)
                        if fn is None:
                            continue
                        ins = fn(e)
                        if incspec is not None:
                            ins.then_inc(incspec[0], incspec[1])
                return body
            block.tensor(mk(self.eng["pe"]))
            block.scalar(mk(self.eng["act"]))
            block.vector(mk(self.eng["dve"]))
            block.gpsimd(mk(self.eng["pool"]))
            block.sync(mk(self.eng["sp"]))
        for E in self.eng.values():
            E.ops = []

    def mm(self, out, lhsT, rhs, start, stop, reads, writes, inc):
        self.op("pe", lambda e: e.matmul(out, lhsT=lhsT, rhs=rhs, start=start, stop=stop),
                reads=reads, writes=writes, inc=True)

    def act(self, out, in_, func, reads, writes, bias=None, scale=None):
        kw = {}
        if bias is not None:
            kw["bias"] = bias
        if scale is not None:
            kw["scale"] = scale
        self.op("act", lambda e: e.activation(out=out, in_=in_, func=func, **kw), reads=reads, writes=writes)

    def tt(self, eng, out, in0, in1, op, reads, writes):
        self.op(eng, lambda e: e.tensor_tensor(out=out, in0=in0, in1=in1, op=op), reads=reads, writes=writes)

    def ts(self, eng, out, in0, s1, s2, op0, op1, reads, writes):
        if op1 is None:
            self.op(eng, lambda e: e.tensor_scalar(out=out, in0=in0, scalar1=s1, scalar2=None, op0=op0),
                    reads=reads, writes=writes)
        else:
            self.op(eng, lambda e: e.tensor_scalar(out=out, in0=in0, scalar1=s1, scalar2=s2, op0=op0, op1=op1),
                    reads=reads, writes=writes)

    def stt(self, eng, out, in0, scalar, in1, op0, op1, reads, writes):
        self.op(eng, lambda e: e.scalar_tensor_tensor(out=out, in0=in0, scalar=scalar, in1=in1, op0=op0, op1=op1),
                reads=reads, writes=writes)

    def cp(self, eng, out, in_, reads, writes):
        if eng == "act":
            self.op("act", lambda e: e.activation(out=out, in_=in_, func=AF.Copy), reads=reads, writes=writes)
        else:
            self.op(eng, lambda e: e.tensor_copy(out=out, in_=in_), reads=reads, writes=writes)

    def memset(self, eng, ap, val, writes):
        self.op(eng, lambda e: e.memset(ap, val), writes=writes)

    def dma(self, eng, out, in_, reads, writes, sem):
        self.op(eng, lambda e: e.dma_start(out=out, in_=in_), reads=reads, writes=writes, dma=sem)


def build_program(NT=8, NL=2, branches="ABCD", do_ffn=True):
    nc = bass.Bass("TRN2", target_bir_lowering=False)
    xT_in = nc.dram_tensor("xT", [DM, SEQ], F32, kind="ExternalInput").ap()
    wpk = nc.dram_tensor("wpk", [2, 128, NW], F32, kind="ExternalInput").ap()
    cst_in = nc.dram_tensor("cst", [128, NCST], F32, kind="ExternalInput").ap()
    ltab_in = nc.dram_tensor("ltab", [2, 128, 1536], F32, kind="ExternalInput").ap()
    praw_in = nc.dram_tensor("praw", [128, NPR], F32, kind="ExternalInput").ap()
    ssmraw_in = nc.dram_tensor("ssmraw", [2, 128, NSR], F32, kind="ExternalInput").ap()
    outT = nc.dram_tensor("outT", [DM, SEQ], F32, kind="ExternalOutput").ap()
    wb = nc.dram_tensor("wb", [2, 128, NW], BF16, kind="Internal").ap()
    xs = nc.dram_tensor("xs", [DM, SEQ], F32, kind="Internal").ap()
    ssw = nc.dram_tensor("ssw", [2, 32, 128, 512], BF16, kind="Internal").ap()
    sst = nc.dram_tensor("sst", [2, 32, 128, 1024], F32, kind="Internal").ap()

    with ExitStack() as st:
        P = Prog(nc, st)

        def sbuf(name, shape, dt, stack=st):
            return stack.enter_context(nc.sbuf_tensor(name, list(shape), dt))

        cst = sbuf("cst_sb", [128, NCST], F32); b_cst = Buf()
        ones_bf = sbuf("ones_bf", [128, 128], BF16); b_ones = Buf()
        WsT = sbuf("WsT", [128, 2, 4, 128], BF16); b_WsT = Buf()
        Kdir = sbuf("Kdir", [128, 2, 4, 128], BF16); b_Kdir = Buf()
        RHO = sbuf("RHO", [128, 2, 32], F32); b_RHO = Buf()
        COSL = sbuf("COSL", [128, 2, 32], F32); b_COSL = Buf()
        SINL = sbuf("SINL", [128, 2, 32], F32)
        neglam = sbuf("neglam", [128, 2], F32); b_neglam = Buf()
        banks = []
        for i in range(8):
            ps = st.enter_context(nc.psum_tensor("ps%d" % i, [128, 512], F32))
            banks.append((ps, Buf("bank%d" % i)))
        PS = Pool_(banks)

        cst_sem = P.dma_sem("ld_cst")
        praw_sem = P.dma_sem("ld_praw")
        raw_sem = P.dma_sem("ld_raw")
        ct_sem = P.dma_sem("st_ct")
        st_sem = P.dma_sem("st_st")
        wst_sem = P.dma_sem("st_wst")
        ct_sem2 = P.dma_sem("st_ct2")
        st_sem2 = P.dma_sem("st_st2")
        wst_sem2 = P.dma_sem("st_wst2")
        b_ssm = [[Buf() for _ in range(6)] for _ in range(2)]
        ssm_sem = (ct_sem, st_sem, wst_sem, raw_sem, ct_sem2, st_sem2, wst_sem2)

        P.dma("sp", cst[:], cst_in[:, :], [], [b_cst], cst_sem)
        P.memset("dve", ones_bf[:], 1.0, [b_ones])

        with ExitStack() as pst:
            praw = sbuf("praw_sb", [128, NPR], F32, pst); b_praw = Buf()
            P.dma("sp", praw[:], praw_in[:, :], [], [b_praw], praw_sem)
            raw2 = sbuf("ssm_raw2", [128, 2, NSR], F32, pst)
            b_raw2 = [Buf(), Buf()]
            raw_sems = [raw_sem, P.dma_sem("ld_raw1")]
            if "D" in branches:
                for l in range(NL):
                    P.dma("sp", raw2[:, l, :], ssmraw_in[l, :, :], [], [b_raw2[l]], raw_sems[l])
            P.wait_all("pool", [(d_.sem, d_.cnt) for d_ in (cst_sem, praw_sem, raw_sems[0], raw_sems[1]) if d_.cnt > 0])
            CH = 8192
            NCH = NW // CH
            b_wbc = [[Buf() for _ in range(NCH)] for _ in range(2)]
            for l in range(1):
                for ci in range(NCH):
                    c0 = ci * CH
                    src = wpk[l, :, c0:c0 + CH].rearrange("p (a b) -> p a b", b=2048)
                    dst = wb[l, :, c0:c0 + CH].rearrange("p (a b) -> p a b", b=2048)
                    P.dma("pool", dst, src, [], [b_wbc[l][ci]], P.dma_sem("cast%d_%d" % (l, ci)))

            for l in range(NL):
                P.tt("dve", WsT[:, l, :, :],
                     praw[:, l * PR_L:l * PR_L + 512].rearrange("p (g t) -> p g t", g=4),
                     praw[:, PR_TRIL:PR_TRIL + 128].unsqueeze(1).to_broadcast([128, 4, 128]),
                     ALU.mult, [b_praw], [b_WsT])
            lt = sbuf("lam_t", [128, 8], F32, pst); b_lt = Buf()
            lp = sbuf("lam_p", [128, 128], F32, pst); b_lp = Buf()
            for l in range(NL):
                lam_init = 0.8 - 0.6 * math.exp(-0.3 * l)
                lq = praw[:, l * PR_L + 512:l * PR_L + 768]
                P.tt("dve", lp[:, 0:64], lq[:, 0:64], lq[:, 64:128], ALU.mult, [b_praw], [b_lp])
                P.tt("dve", lp[:, 64:128], lq[:, 128:192], lq[:, 192:256], ALU.mult, [b_praw], [b_lp])
                P.op("dve", lambda e: e.tensor_reduce(out=lt[:, 0:2], in_=lp[:, :].rearrange("p (a b) -> p a b", a=2),
                                                      axis=AX.X, op=ALU.add), reads=[b_lp], writes=[b_lt])
                P.act(lt[:, 2:4], lt[:, 0:2], AF.Exp, [b_lt], [b_lt])
                P.tt("dve", lt[:, 4:5], lt[:, 3:4], lt[:, 2:3], ALU.subtract, [b_lt], [b_lt])
                P.ts("dve", neglam[:, l:l + 1], lt[:, 4:5], -lam_init, None, ALU.add, None, [b_lt], [b_neglam])

            if "D" in branches:
                _ssm_prologue(nc, P, pst, sbuf, NL, ssmraw_in, praw, b_praw, cst, b_cst, ssw, sst, ssm_sem, b_ssm,
                              Kdir, b_Kdir, RHO, b_RHO, PS, (COSL, SINL, b_COSL, raw2, b_raw2))
            P.barrier([cst_sem, praw_sem, raw_sems[0], raw_sems[1], ct_sem, st_sem, wst_sem, ct_sem2, st_sem2, wst_sem2])
            P.replay()

        for l in range(1, NL):
            for ci in range(NCH):
                c0 = ci * CH
                src = wpk[l, :, c0:c0 + CH].rearrange("p (a b) -> p a b", b=2048)
                dst = wb[l, :, c0:c0 + CH].rearrange("p (a b) -> p a b", b=2048)
                P.dma("pool", dst, src, [], [b_wbc[l][ci]], P.dma_sem("cast%d_%d" % (l, ci)))
        ring = sbuf("ring", [128, NSLOT, SLOT], BF16)
        ring_b = [Buf("ring%d" % i) for i in range(NSLOT)]
        ring_sem = [P.dma_sem("ring%d" % i) for i in range(NSLOT)]
        sr_bb = sbuf("sr_bb", [128, NSS, 256], BF16)
        sr_cc = sbuf("sr_cc", [128, NSS, 256], BF16)
        sr_tab = sbuf("sr_tab", [128, NSS, 1024], F32)
        KT = sbuf("KT", [128, 4, SEQ], BF16)
        b_KT = [[Buf() for _ in range(8)] for _ in range(4)]
        Vc = sbuf("Vc", [128, 32, 512], BF16)
        b_Vc = [Buf() for _ in range(32)]
        x = sbuf("x", [128, 8, TT], F32)
        b_x = [Buf("x%d" % c) for c in range(8)]
        hT = sbuf("hT", [128, 8, TT], BF16)
        b_h = [Buf("h%d" % c) for c in range(8)]
        arena = sbuf("arena", [128, 28, TT], BF16)
        b_ar = [Buf("ar%d" % c) for c in range(28)]
        qpad = sbuf("qpad", [128, 4, 2, 2, 256], BF16)
        b_qp = [Buf("qp%d" % h) for h in range(4)]
        P.memset("pool", qpad[:], 0.0, b_qp)
        NTF = 8
        tfp = sbuf("tfp", [128, NTF, 514], F32)
        TF = Pool_([(tfp[:, i, :], Buf("tf%d" % i)) for i in range(NTF)])
        NTB = 7
        tbp = sbuf("tbp", [128, NTB, TT], BF16)
        TB = Pool_([(tbp[:, i, :], Buf("tb%d" % i)) for i in range(NTB)])
        ltab = sbuf("ltab_sb", [128, 1536], F32); b_ltab = Buf()
        zcarry = sbuf("zcarry", [128, 4, 2], F32); b_zc = Buf()
        scarry = sbuf("scarry", [128, 32], F32); b_scarry = Buf()
        zl = sbuf("zl", [128, 32], F32); b_zl = Buf()
        ccb = sbuf("ccb", [128, 64], F32); b_ccb = Buf()
        small = sbuf("small", [128, 16], F32); b_small = Buf()
        xld_c = [P.dma_sem("xld%d" % c) for c in range(8)]
        xst_c = [P.dma_sem("xst%d" % c) for c in range(8)]
        lt_sem = P.dma_sem("ltab")

        seq = []
        for l in range(NL):
            for i in range(NT):
                for name, size in REC:
                    seq.append((l, name))
        def used(name):
            if name.startswith("A"):
                return "A" in branches
            if name.startswith("B") and not name.startswith("BR"):
                return "B" in branches
            if name.startswith("C"):
                return "C" in branches
            if name in ("Du", "Wglu"):
                return "D" in branches
            if name.startswith("F") or (name.startswith("D") and name not in ("Du",)):
                return do_ffn
            return True
        seq = [(l, n) for (l, n) in seq if used(n)]
        state = {"next_load": 0, "next_get": 0, "sload": 0, "sget": 0}

        def ring_get(expect):
            r = state["next_get"]
            assert seq[r][1] == expect, (seq[r], expect)
            while state["next_load"] < len(seq) and state["next_load"] <= r + NSLOT - 2:
                q = state["next_load"]
                l, name = seq[q]
                off, size = REC_OFF[name]
                s = q % NSLOT
                cbs = [b_wbc[l][ci] for ci in range(off // CH, (off + size - 1) // CH + 1)]
                P.dma("sp", ring[:, s, 0:size], wb[l, :, off:off + size], cbs, [ring_b[s]], ring_sem[s])
                state["next_load"] += 1
            state["next_get"] += 1
            s = r % NSLOT
            return ring[:, s, :], ring_b[s]

        sseq = [(l, g) for l in range(NL) for i in range(NT) for g in range(32)] if "D" in branches else []

        class Stream:
            def __init__(self, name, issue):
                self.bufs = [Buf() for _ in range(NSS)]
                self.sems = [P.dma_sem("%s%d" % (name, k)) for k in range(NSS)]
                self.nload = 0
                self.nget = 0
                self.issue = issue

            def get(self):
                r = self.nget
                while self.nload < len(sseq) and self.nload <= r + NSS - 2:
                    q = self.nload
                    self.issue(sseq[q], q % NSS, self.bufs[q % NSS], self.sems[q % NSS])
                    self.nload += 1
                self.nget += 1
                return r % NSS

        st_bb = Stream("srbb", lambda it, s_, b, sem: P.dma("sp", sr_bb[:, s_, :], ssw[it[0], it[1], :, 0:256],
                                                            [b_ssm[it[0]][4], b_ssm[it[0]][5]], [b], sem))
        st_cc = Stream("srcc", lambda it, s_, b, sem: P.dma("sp", sr_cc[:, s_, :], ssw[it[0], it[1], :, 256:512],
                                                            [b_ssm[it[0]][4], b_ssm[it[0]][5]], [b], sem))
        st_tab = Stream("srtab", lambda it, s_, b, sem: P.dma("sp", sr_tab[:, s_, :], sst[it[0], it[1], :, :],
                                                              b_ssm[it[0]][0:4], [b], sem))

        def proj_fm(slot_view_fn, rb, bank, bbuf, hbufs=None):
            for k in range(8):
                P.mm(bank[:, :], slot_view_fn(k), hT[:, k, :], k == 0, k == 7,
                     [rb, b_h[k]], [bbuf], k == 7)

        def rmsnorm_stats(eps, count, scale2=1.0):
            bank, bb = PS.get()
            sqs = []
            for c in range(8):
                sq = TB.get()
                P.act(sq[0], x[:, c, :], AF.Square, [b_x[c]], [sq[1]])
                P.mm(bank[:, :], ones_bf[:, :], sq[0], c == 0, c == 7, [b_ones, sq[1]], [bb], c == 7)
                TB.put(sq)
            rt = TF.get()
            P.act(rt[0][:, 0:TT], bank[:, :], AF.Ln, [bb], [rt[1]], bias=eps / scale2, scale=1.0 / (count * scale2))
            PS.put((bank, bb))
            P.act(rt[0][:, 0:TT], rt[0][:, 0:TT], AF.Exp, [rt[1]], [rt[1]], scale=-0.5)
            return rt

        def norm_to_h(gcol):
            rt = rmsnorm_stats(1e-6, 1024.0)
            for c in range(8):
                eng = "dve"
                P.stt(eng, hT[:, c, :], x[:, c, :], cst[:, gcol + c:gcol + c + 1], rt[0][:, 0:TT], ALU.mult, ALU.mult,
                      [b_x[c], b_cst, rt[1]], [b_h[c]])
            TF.put(rt)

        for l in range(NL):
            lam_init = 0.8 - 0.6 * math.exp(-0.3 * l)
            cb = l * CL
            P.new_epoch("L%d" % l)
            P.dma("sp", ltab[:], ltab_in[l, :, :], [], [b_ltab], lt_sem)
            P.memset("dve", zcarry[:], 0.0, [b_zc])
            P.memset("dve", scarry[:], 0.0, [b_scarry])
            for i in range(NT):
                t0 = i * TT
                src = (xT_in if l == 0 else xs).rearrange("(c p) t -> p c t", p=128)[:, :, t0:t0 + TT]
                for c in range(8):
                    P.dma("sp", x[:, c, :], src[:, c, :], [], [b_x[c]], xld_c[c])
                norm_to_h(cb + C_GMIX)

                do_D = "D" in branches
                if do_D:
                    slot, rb = ring_get("Du")
                    sv = slot[:, 0:4096].rearrange("p (k n) -> p k n", k=8)
                    for j in range(4):
                        bk = PS.get()
                        proj_fm(lambda k, j=j: sv[:, k, j * 128:(j + 1) * 128], rb, bk[0], bk[1])
                        P.cp("act" if j % 2 else "dve", arena[:, 16 + j, :], bk[0][:, :], [bk[1]], [b_ar[16 + j]])
                        PS.put(bk)
                else:
                    for j in range(4):
                        P.memset("pool", arena[:, 12 + j, :], 0.0, [b_ar[12 + j]])

                def abc_units(l=l, i=i, cb=cb, t0=t0):
                    for j in range(4):
                        if "A" not in branches:
                            P.memset("pool", arena[:, 0 + j, :], 0.0, [b_ar[0 + j]])
                            continue
                        slot, rb = ring_get("A%d" % j)
                        sv = slot[:, 0:3072].rearrange("p (k s n) -> p k s n", k=8, s=3)
                        pb = PS.get(); pc = PS.get(); px = PS.get()
                        for s_, bk in ((0, pb), (1, pc), (2, px)):
                            proj_fm(lambda k, s_=s_: sv[:, k, s_, :], rb, bk[0], bk[1])
                        axs = TF.get(); z = TF.get(); pbs = TF.get()
                        P.cp("act", axs[0][:, 0:TT], px[0][:, :], [px[1]], [axs[1]])
                        PS.put(px)
                        P.cp("act", pbs[0][:, 0:TT], pb[0][:, :], [pb[1]], [pbs[1]])
                        PS.put(pb)
                        P.tt("dve", z[0][:, 2:514], pc[0][:, :], axs[0][:, 0:TT], ALU.mult, [pc[1], axs[1]], [z[1]])
                        PS.put(pc)
                        acc = axs
                        P.cp("dve", z[0][:, 0:2], zcarry[:, j, :], [b_zc], [z[1]])
                        cw = cb + C_CONVW
                        P.ts("dve", acc[0][:, 0:TT], z[0][:, 2:514], cst[:, cw + 8 + j:cw + 9 + j],
                             cst[:, cb + C_CONVB + j:cb + C_CONVB + j + 1], ALU.mult, ALU.add, [z[1], b_cst], [acc[1]])
                        P.stt("dve", acc[0][:, 0:TT], z[0][:, 1:513], cst[:, cw + 4 + j:cw + 5 + j], acc[0][:, 0:TT],
                              ALU.mult, ALU.add, [z[1], b_cst, acc[1]], [acc[1]])
                        P.stt("dve", acc[0][:, 0:TT], z[0][:, 0:512], cst[:, cw + j:cw + 1 + j], acc[0][:, 0:TT],
                              ALU.mult, ALU.add, [z[1], b_cst, acc[1]], [acc[1]])
                        P.tt("dve", arena[:, 0 + j, :], pbs[0][:, 0:TT], acc[0][:, 0:TT], ALU.mult, [pbs[1], acc[1]], [b_ar[0 + j]])
                        P.cp("dve", zcarry[:, j, :], z[0][:, 512:514], [z[1]], [b_zc])
                        TF.put(z); TF.put(acc); TF.put(pbs)
                        yield
                    if "B" not in branches:
                        for j in range(4):
                            P.memset("pool", arena[:, 4 + j, :], 0.0, [b_ar[4 + j]])
                    else:
                        slot, rb = ring_get("Bu")
                        sv = slot[:, 0:4096].rearrange("p (k n) -> p k n", k=8)
                        for j in range(4):
                            bk = PS.get()
                            proj_fm(lambda k, j=j: sv[:, k, j * 128:(j + 1) * 128], rb, bk[0], bk[1])
                            P.act(arena[:, 4 + j, :], bk[0][:, :], AF.Gelu_apprx_tanh, [bk[1]], [b_ar[4 + j]])
                            PS.put(bk)
                            yield
                        slot, rb = ring_get("Bv")
                        sv = slot[:, 0:4096].rearrange("p (k n) -> p k n", k=8)
                        for tb in range(4):
                            bk = PS.get()
                            for k in range(8):
                                P.mm(bk[0][:, :], hT[:, k, tb * 128:(tb + 1) * 128], sv[:, k, :], k == 0, k == 7,
                                     [rb, b_h[k]], [bk[1]], k == 7)
                            vg = TF.get()
                            P.act(vg[0][:, 0:TT], bk[0][:, :], AF.Gelu_apprx_tanh, [bk[1]], [vg[1]])
                            PS.put(bk)
                            P.op("dve", lambda e, vg=vg: e.bn_stats(out=small[:, 0:6], in_=vg[0][:, 0:TT]),
                                 reads=[vg[1]], writes=[b_small])
                            P.op("dve", lambda e: e.bn_aggr(out=small[:, 8:10], in_=small[:, 0:6]),
                                 reads=[b_small], writes=[b_small])
                            P.act(small[:, 10:11], small[:, 9:10], AF.Ln, [b_small], [b_small], bias=1e-5)
                            P.act(small[:, 11:12], small[:, 10:11], AF.Exp, [b_small], [b_small], scale=-0.5)
                            P.ts("dve", vg[0][:, 0:TT], vg[0][:, 0:TT], small[:, 8:9], small[:, 11:12], ALU.subtract, ALU.mult,
                                 [vg[1], b_small], [vg[1]])
                            P.tt("pool", vg[0][:, 0:TT], vg[0][:, 0:TT], ltab[:, 0:512], ALU.mult, [vg[1], b_ltab], [vg[1]])
                            P.tt("pool", arena[:, 20 + tb, :], vg[0][:, 0:TT], ltab[:, 512:1024], ALU.add,
                                 [vg[1], b_ltab], [b_ar[20 + tb]])
                            TF.put(vg)
                            yield
                        for g in range(4):
                            bk = PS.get()
                            for tb in range(4):
                                P.mm(bk[0][:, tb * 128:(tb + 1) * 128], arena[:, 20 + tb, g * 128:(g + 1) * 128],
                                     WsT[:, l, g, :], True, True, [b_ar[20 + tb], b_WsT], [bk[1]], tb == 3)
                            tm = TF.get()
                            P.tt("dve", tm[0][:, 0:TT].rearrange("p (a b) -> p a b", a=4),
                                 bk[0][:, :].rearrange("p (a b) -> p a b", a=4),
                                 ltab[:, 1024 + g * 128:1024 + (g + 1) * 128].unsqueeze(1).to_broadcast([128, 4, 128]),
                                 ALU.add, [bk[1], b_ltab], [tm[1]])
                            PS.put(bk)
                            P.tt("dve", arena[:, 4 + g, :], tm[0][:, 0:TT], arena[:, 4 + g, :], ALU.mult,
                                 [tm[1], b_ar[4 + g]], [b_ar[4 + g]])
                            TF.put(tm)
                            yield
                    if "C" in branches:
                        slot, rb = ring_get("Cq")
                        sv = slot[:, 0:4096].rearrange("p (k n) -> p k n", k=8)
                        for h in range(4):
                            bk = PS.get()
                            proj_fm(lambda k, h=h: sv[:, k, h * 128:(h + 1) * 128], rb, bk[0], bk[1])
                            for hf_ in range(2):
                                P.cp("act", qpad[0:64, h, hf_, 0, :], bk[0][0:64, hf_ * 256:(hf_ + 1) * 256], [bk[1]], [b_qp[h]])
                                P.cp("dve", qpad[64:128, h, hf_, 1, :], bk[0][64:128, hf_ * 256:(hf_ + 1) * 256], [bk[1]], [b_qp[h]])
                            PS.put(bk)
                            yield
                        slot, rb = ring_get("Ck")
                        sv = slot[:, 0:4096].rearrange("p (k n) -> p k n", k=8)
                        for h in range(4):
                            bk = PS.get()
                            proj_fm(lambda k, h=h: sv[:, k, h * 128:(h + 1) * 128], rb, bk[0], bk[1])
                            P.cp("dve", KT[:, h, t0:t0 + TT], bk[0][:, :], [bk[1]], [b_KT[h][i]])
                            PS.put(bk)
                            yield
                        slot, rb = ring_get("Cv")
                        sv = slot[:, 0:4096].rearrange("p (k n) -> p k n", k=8)
                        for tb in range(4):
                            bk = PS.get()
                            for k in range(8):
                                P.mm(bk[0][:, :], hT[:, k, tb * 128:(tb + 1) * 128], sv[:, k, :], k == 0, k == 7,
                                     [rb, b_h[k]], [bk[1]], k == 7)
                            P.cp("act", Vc[:, 4 * i + tb, :], bk[0][:, :], [bk[1]], [b_Vc[4 * i + tb]])
                            PS.put(bk)
                            yield

                def attn_units(l=l, i=i, cb=cb, lam_init=lam_init):
                    if "C" not in branches:
                        for j in range(4):
                            P.memset("pool", arena[:, 8 + j, :], 0.0, [b_ar[8 + j]])
                        return
                    pend_e2 = []
                    for h in range(4):
                        for hf in range(2):
                            qlo = 256 * hf
                            kbase = 4 * i + 2 * hf
                            nkt = kbase + 2
                            OB = PS.get(); LB = PS.get()

                            def emit_s(kt, h=h, qlo=qlo, kbase=kbase):
                                j = kt - kbase
                                q0 = 128 * max(j, 0)
                                sb_ = PS.get()
                                kb = b_KT[h][kt // 4]
                                qb = b_qp[h]
                                hf_ = qlo // 256
                                if q0 == 0:
                                    P.mm(sb_[0][:, 0:512], KT[:, h, kt * 128:(kt + 1) * 128],
                                         qpad[:, h, hf_, :, :].rearrange("p a b -> p (a b)"), True, True, [kb, qb], [sb_[1]], True)
                                else:
                                    P.mm(sb_[0][:, q0:256], KT[:, h, kt * 128:(kt + 1) * 128],
                                         qpad[:, h, hf_, 0, q0:256], True, True, [kb, qb], [sb_[1]], True)
                                    P.mm(sb_[0][:, 256 + q0:512], KT[:, h, kt * 128:(kt + 1) * 128],
                                         qpad[:, h, hf_, 1, q0:256], True, True, [kb, qb], [sb_[1]], True)
                                return (sb_, j, q0)

                            sq_ = [emit_s(0), emit_s(1)]
                            for kt in range(nkt):
                                sb_, j, q0 = sq_.pop(0)
                                p = TB.get()
                                if q0 == 0:
                                    P.act(p[0][:, 0:512], sb_[0][:, 0:512], AF.Exp, [sb_[1]], [p[1]], scale=0.125)
                                else:
                                    P.act(p[0][:, q0:256], sb_[0][:, q0:256], AF.Exp, [sb_[1]], [p[1]], scale=0.125)
                                    P.act(p[0][:, 256 + q0:512], sb_[0][:, 256 + q0:512], AF.Exp, [sb_[1]], [p[1]], scale=0.125)
                                PS.put(sb_)
                                if kt == 1 and pend_e2:
                                    pend_e2.pop(0)()
                                if kt + 2 < nkt:
                                    sq_.append(emit_s(kt + 2))
                                if j >= 0:
                                    P.memset("pool", p[0][64:128, q0:q0 + 64], 0.0, [p[1]])
                                    P.memset("pool", p[0][64:128, 256 + q0:256 + q0 + 64], 0.0, [p[1]])
                                first = kt == 0
                                last = kt == nkt - 1
                                if q0 == 0:
                                    rngs = [(0, 512)]
                                else:
                                    rngs = [(q0, 256), (256 + q0, 512)]
                                for ri, (c0, c1) in enumerate(rngs):
                                    lst = last and ri == len(rngs) - 1
                                    P.mm(OB[0][:, c0:c1], Vc[:, kt, h * 128:(h + 1) * 128], p[0][:, c0:c1],
                                         first and ri == 0, lst, [b_Vc[kt], p[1]], [OB[1]], True)
                                    P.mm(LB[0][:, c0:c1], ones_bf[:, :], p[0][:, c0:c1],
                                         first and ri == 0, lst, [b_ones, p[1]], [LB[1]], True)
                                TB.put(p)
                                yield
                            r = TF.get()
                            P.act(r[0][:, 0:TT], LB[0][:, :], AF.Ln, [LB[1]], [r[1]])
                            PS.put(LB)
                            P.act(r[0][:, 0:TT], r[0][:, 0:TT], AF.Exp, [r[1]], [r[1]], scale=-1.0)
                            P.tt("dve", r[0][:, 0:TT], OB[0][:, :], r[0][:, 0:TT], ALU.mult, [OB[1], r[1]], [r[1]])
                            PS.put(OB)
                            o = TF.get()
                            P.stt("dve", o[0][:, 0:256], r[0][:, 256:512], neglam[:, l:l + 1], r[0][:, 0:256], ALU.mult, ALU.add,
                                  [r[1], b_neglam], [o[1]])
                            TF.put(r)
                            sq = TB.get()
                            P.tt("pool", sq[0][:, 0:256], o[0][:, 0:256], o[0][:, 0:256], ALU.mult, [o[1]], [sq[1]])

                            def e2(o=o, sq=sq, h=h, qlo=qlo):
                                bk = PS.get()
                                P.mm(bk[0][:, 0:256], ones_bf[:, :], sq[0][:, 0:256], True, True, [b_ones, sq[1]], [bk[1]], True)
                                TB.put(sq)
                                rs = TF.get()
                                c2 = (1.0 - lam_init) ** 2
                                P.act(rs[0][:, 0:256], bk[0][:, 0:256], AF.Ln, [bk[1]], [rs[1]], bias=1e-5 / c2,
                                      scale=1.0 / (128.0 * c2))
                                PS.put(bk)
                                P.act(rs[0][:, 0:256], rs[0][:, 0:256], AF.Exp, [rs[1]], [rs[1]], scale=-0.5)
                                P.stt("dve", arena[:, 8 + h, qlo:qlo + 256], o[0][:, 0:256],
                                      cst[:, cb + C_SUBLN:cb + C_SUBLN + 1], rs[0][:, 0:256], ALU.mult, ALU.mult,
                                      [o[1], b_cst, rs[1]], [b_ar[8 + h]])
                                TF.put(o); TF.put(rs)
                            pend_e2.append(e2)
                            yield
                    while pend_e2:
                        pend_e2.pop(0)()
                        yield

                def all_units():
                    yield from abc_units()
                    yield from attn_units()

                n_units = (4 if "A" in branches else 0) + (12 if "B" in branches else 0) + \
                          ((12 + 32 * i + 40) if "C" in branches else 0)
                n_pull = max(1, -(-n_units // 35))
                units = all_units()

                if do_D:
                    dst_ = {}
                    dY = {"Y": None}
                    for it in range(32 + 3):
                        g = it - 1
                        if 0 <= g < 32:
                            d = dst_[g]
                            ts_ = st_tab.get()
                            tb_ = st_tab.bufs[ts_]
                            a = TF.get(); b = TF.get()
                            P.tt("dve", a[0][:, 0:TT], d["S"][0][:, :], sr_tab[:, ts_, 0:512], ALU.mult, [d["S"][1], tb_], [a[1]])
                            P.tt("dve", b[0][:, 0:TT], d["Sw"][0][:, :], sr_tab[:, ts_, 512:1024], ALU.mult, [d["Sw"][1], tb_], [b[1]])
                            PS.put(d["S"]); PS.put(d["Sw"])
                            P.tt("pool", a[0][:, 0:TT], a[0][:, 0:TT], b[0][:, 0:TT], ALU.add, [a[1], b[1]], [a[1]])
                            TF.put(b)
                            d["a"] = a; d["ts"] = ts_
                        g = it - 2
                        if 0 <= g < 32:
                            d = dst_[g]
                            a = d["a"]; ts_ = d["ts"]; tb_ = st_tab.bufs[ts_]
                            Z = TF.get()
                            P.op("dve", lambda e, a=a, Z=Z, g=g, l=l: e.tensor_tensor_scan(
                                out=Z[0][:, 0:TT], data0=RHO[:, l, g:g + 1].to_broadcast([128, TT]), data1=a[0][:, 0:TT],
                                initial=scarry[:, g:g + 1], op0=ALU.mult, op1=ALU.add),
                                reads=[a[1], b_RHO, b_scarry], writes=[Z[1]])
                            TF.put(a)
                            zc = TB.get(); zs = TB.get()
                            P.tt("pool", zc[0], Z[0][:, 0:TT], sr_tab[:, ts_, 0:512], ALU.mult, [Z[1], tb_], [zc[1]])
                            P.tt("dve", zs[0], Z[0][:, 0:TT], sr_tab[:, ts_, 512:1024], ALU.mult, [Z[1], tb_], [zs[1]])
                            P.cp("pool", zl[:, g:g + 1], Z[0][:, TT - 1:TT], [Z[1]], [b_zl])
                            TF.put(Z)
                            d["zc"] = zc; d["zs"] = zs
                        g = it - 3
                        if 0 <= g < 32:
                            d = dst_.pop(g)
                            j = g // 8
                            sc_ = st_cc.get()
                            cbuf = st_cc.bufs[sc_]
                            if g % 8 == 0:
                                dY["Y"] = PS.get()
                            Y = dY["Y"]
                            P.mm(Y[0][:, :], sr_cc[:, sc_, 0:128], d["zc"][0], g % 8 == 0, False, [cbuf, d["zc"][1]], [Y[1]], True)
                            P.mm(Y[0][:, :], sr_cc[:, sc_, 128:256], d["zs"][0], False, False, [cbuf, d["zs"][1]], [Y[1]], True)
                            TB.put(d["zc"]); TB.put(d["zs"])
                            if g % 8 == 7:
                                P.mm(Y[0][:, :], Kdir[:, l, j, :], arena[:, 16 + j, :], False, True,
                                     [b_Kdir, b_ar[16 + j]], [Y[1]], True)
                                P.act(arena[:, 24 + j, :], Y[0][:, :], AF.Gelu_apprx_tanh, [Y[1]], [b_ar[24 + j]])
                                PS.put(Y)
                        g = it
                        if g < 32:
                            j = g // 8
                            sb = st_bb.get()
                            S = PS.get(); Sw = PS.get()
                            P.mm(S[0][:, :], sr_bb[:, sb, 0:128], arena[:, 16 + j, :], True, True,
                                 [st_bb.bufs[sb], b_ar[16 + j]], [S[1]], True)
                            P.mm(Sw[0][:, :], sr_bb[:, sb, 128:256], arena[:, 16 + j, :], True, True,
                                 [st_bb.bufs[sb], b_ar[16 + j]], [Sw[1]], True)
                            dst_[g] = {"S": S, "Sw": Sw}
                        for _ in range(n_pull):
                            next(units, None)
                    P.tt("dve", ccb[:, 0:32], zl[:, :], COSL[:, l, :], ALU.mult, [b_zl, b_COSL], [b_ccb])
                    P.tt("dve", ccb[:, 32:64], zl[:, :], SINL[:, l, :], ALU.mult, [b_zl, b_COSL], [b_ccb])
                    cbk = PS.get()
                    P.mm(cbk[0][:, 0:32], cst[:, C_IDENT:C_IDENT + 128], ccb[:, 0:32], True, False, [b_cst, b_ccb], [cbk[1]], True)
                    P.mm(cbk[0][:, 0:32], cst[:, C_JMAT:C_JMAT + 128], ccb[:, 32:64], False, True, [b_cst, b_ccb], [cbk[1]], True)
                    P.cp("act", scarry[:, :], cbk[0][:, 0:32], [cbk[1]], [b_scarry])
                    PS.put(cbk)
                for _ in units:
                    pass

                if do_D:
                    slot, rb = ring_get("Wglu")
                    sv = slot[:, 0:2048].rearrange("p (c n) -> p c n", c=4)
                    for j in range(4):
                        bk = PS.get()
                        for c in range(4):
                            P.mm(bk[0][:, :], sv[:, c, j * 128:(j + 1) * 128], arena[:, 24 + c, :], c == 0, c == 3,
                                 [rb, b_ar[24 + c]], [bk[1]], c == 3)
                        sg = TF.get()
                        P.act(sg[0][:, 0:TT], bk[0][:, :], AF.Sigmoid, [bk[1], b_cst], [sg[1]],
                              bias=cst[:, cb + C_BGLU + j:cb + C_BGLU + j + 1])
                        PS.put(bk)
                        P.tt("dve", arena[:, 12 + j, :], arena[:, 24 + j, :], sg[0][:, 0:TT], ALU.mult,
                             [b_ar[24 + j], sg[1]], [b_ar[12 + j]])
                        TF.put(sg)

                for m in range(8):
                    gslot, grb = ring_get("G%d" % m)
                    gv = gslot[:, 0:4096].rearrange("p (k b n) -> p k b n", k=8, b=4)
                    bslot, brb = ring_get("BR%d" % m)
                    bv = bslot[:, 0:2048].rearrange("p (b c n) -> p b c n", b=4, c=4)
                    prods = []
                    for b in range(4):
                        gk = PS.get()
                        proj_fm(lambda k, b=b: gv[:, k, b, :], grb, gk[0], gk[1])
                        gt = TF.get()
                        P.act(gt[0][:, 0:TT], gk[0][:, :], AF.Sigmoid, [gk[1]], [gt[1]])
                        PS.put(gk)
                        rk = PS.get()
                        for c in range(4):
                            P.mm(rk[0][:, :], bv[:, b, c, :], arena[:, 4 * b + c, :], c == 0, c == 3,
                                 [brb, b_ar[4 * b + c]], [rk[1]], c == 3)
                        P.tt("dve", gt[0][:, 0:TT], rk[0][:, :], gt[0][:, 0:TT], ALU.mult, [rk[1], gt[1]], [gt[1]])
                        PS.put(rk)
                        prods.append(gt)
                    P.tt("pool", prods[0][0][:, 0:TT], prods[0][0][:, 0:TT], prods[1][0][:, 0:TT], ALU.add,
                         [prods[0][1], prods[1][1]], [prods[0][1]])
                    P.tt("pool", prods[2][0][:, 0:TT], prods[2][0][:, 0:TT], prods[3][0][:, 0:TT], ALU.add,
                         [prods[2][1], prods[3][1]], [prods[2][1]])
                    P.tt("pool", arena[:, 16 + m, :], prods[0][0][:, 0:TT], prods[2][0][:, 0:TT], ALU.add,
                         [prods[0][1], prods[2][1]], [b_ar[16 + m]])
                    for pr in prods:
                        TF.put(pr)
                for half in range(2):
                    slot, rb = ring_get("WO%d" % half)
                    sv = slot[:, 0:4096].rearrange("p (k n) -> p k n", k=8)
                    for mm_ in range(4):
                        m = half * 4 + mm_
                        bk = PS.get()
                        for k in range(8):
                            P.mm(bk[0][:, :], sv[:, k, mm_ * 128:(mm_ + 1) * 128], arena[:, 16 + k, :], k == 0, k == 7,
                                 [rb, b_ar[16 + k]], [bk[1]], k == 7)
                        P.tt("dve", x[:, m, :], x[:, m, :], bk[0][:, :], ALU.add, [b_x[m], bk[1]], [b_x[m]])
                        PS.put(bk)

                if do_ffn:
                    norm_to_h(cb + C_GFFN)
                    for fp in range(11):
                        slot, rb = ring_get("F%d" % fp)
                        sv = slot[:, 0:4096].rearrange("p (f s k n) -> p f s k n", f=2, s=2, k=8)
                        for fi in range(2):
                            f = 2 * fp + fi
                            gk = PS.get(); uk = PS.get()
                            proj_fm(lambda k, fi=fi: sv[:, fi, 0, k, :], rb, gk[0], gk[1])
                            proj_fm(lambda k, fi=fi: sv[:, fi, 1, k, :], rb, uk[0], uk[1])
                            sg = TF.get()
                            P.act(sg[0][:, 0:TT], gk[0][:, :], AF.Silu, [gk[1]], [sg[1]])
                            PS.put(gk)
                            P.tt("dve", arena[:, f, :], uk[0][:, :], sg[0][:, 0:TT], ALU.mult, [uk[1], sg[1]], [b_ar[f]])
                            PS.put(uk); TF.put(sg)
                    for m in range(8):
                        slot, rb = ring_get("D%d" % m)
                        sv = slot[:, 0:2816].rearrange("p (f n) -> p f n", f=22)
                        bk = PS.get()
                        for f in range(22):
                            P.mm(bk[0][:, :], sv[:, f, :], arena[:, f, :], f == 0, f == 21, [rb, b_ar[f]], [bk[1]], f == 21)
                        P.tt("dve", x[:, m, :], x[:, m, :], bk[0][:, :], ALU.add, [b_x[m], bk[1]], [b_x[m]])
                        PS.put(bk)
                        if l < NL - 1:
                            P.dma("pool", xs.rearrange("(c p) t -> p c t", p=128)[:, m, t0:t0 + TT], x[:, m, :],
                                  [b_x[m]], [], xst_c[m])

                if l == NL - 1:
                    rt = rmsnorm_stats(1e-6, 1024.0)
                    dstv = outT.rearrange("(c p) t -> p c t", p=128)
                    for c in range(8):
                        P.stt("dve", x[:, c, :], x[:, c, :],
                              cst[:, C_GFINAL + c:C_GFINAL + c + 1], rt[0][:, 0:TT],
                              ALU.mult, ALU.mult, [b_x[c], b_cst, rt[1]], [b_x[c]])
                        P.dma("pool", dstv[:, c, t0:t0 + TT], x[:, c, :], [b_x[c]], [], xst_c[c])
                    TF.put(rt)
                elif not do_ffn:
                    dstv = xs.rearrange("(c p) t -> p c t", p=128)
                    for c in range(8):
                        P.dma("pool", dstv[:, c, t0:t0 + TT], x[:, c, :], [b_x[c]], [], xst_c[c])
        P.wait_all("sp", [(d_.sem, d_.cnt) for d_ in xst_c])
        P.replay()
    return nc


def _ssm_prologue(nc, P, pst, sbuf, NL, ssmraw_in, praw, b_praw, cst, b_cst, ssw, sst, ssm_sem, b_ssm,
                  Kdir, b_Kdir, RHO, b_RHO, PS, ld_sem):
    NWK = 16
    wk = sbuf("ssm_wk", [128, NWK, 512], F32, pst)
    b_wk = [Buf() for _ in range(NWK)]
    bbN = sbuf("bbN", [128, 2, 512], F32, pst); b_bbN = Buf()
    ccN = sbuf("ccN", [128, 2, 512], F32, pst); b_ccN = Buf()
    bbT = sbuf("bbT", [128, 2, 512], F32, pst); b_bbT = Buf()
    wst_l = [(sbuf("wst%d" % k, [128, 8, 512], BF16, pst), Buf()) for k in range(2)]
    Ct_l = [(sbuf("Ctab%d" % k, [128, 8, 512], F32, pst), Buf()) for k in range(2)]
    St_l = [(sbuf("Stab%d" % k, [128, 8, 512], F32, pst), Buf()) for k in range(2)]
    tm1 = sbuf("tm1", [128, 8, 256], F32, pst); b_tm1 = Buf()
    tm2 = sbuf("tm2", [128, 8, 256], F32, pst); b_tm2 = Buf()

    def W(i, n):
        return wk[:, i, 0:n]

    def coef(n, are, aim, ldt, keep_trig):
        b = b_wk
        DT, T, TH, SN, CS, RH, LRE, LIM, NR, DEN, CRE, CIM, T2 = range(13)
        rd = [b_raw]
        P.act(W(DT, n), ldt, AF.Exp, rd, [b[DT]])
        P.tt("dve", W(T, n), W(DT, n), are, ALU.mult, [b[DT]] + rd, [b[T]])
        P.act(W(RH, n), W(T, n), AF.Exp, [b[T]], [b[RH]])
        P.tt("dve", W(TH, n), W(DT, n), aim, ALU.mult, [b[DT]] + rd, [b[TH]])
        P.ts("dve", W(T, n), W(TH, n), 1.0 / (2 * PI), MAGIC, ALU.mult, ALU.add, [b[TH]], [b[T]])
        P.ts("dve", W(T, n), W(T, n), MAGIC, -2 * PI, ALU.subtract, ALU.mult, [b[T]], [b[T]])
        P.tt("dve", W(TH, n), W(TH, n), W(T, n), ALU.add, [b[TH], b[T]], [b[TH]])
        P.ts("dve", W(TH, n), W(TH, n), -3.14159, 3.14159, ALU.max, ALU.min, [b[TH]], [b[TH]])
        P.act(W(SN, n), W(TH, n), AF.Sin, [b[TH]], [b[SN]])
        P.ts("dve", W(T2, n), W(TH, n), PI / 2, None, ALU.add, None, [b[TH]], [b[T2]])
        P.ts("dve", W(T, n), W(T2, n), PI, -2 * PI, ALU.is_gt, ALU.mult, [b[T2]], [b[T]])
        P.tt("dve", W(T2, n), W(T2, n), W(T, n), ALU.add, [b[T2], b[T]], [b[T2]])
        P.ts("dve", W(T2, n), W(T2, n), -3.14159, 3.14159, ALU.max, ALU.min, [b[T2]], [b[T2]])
        P.act(W(CS, n), W(T2, n), AF.Sin, [b[T2]], [b[CS]])
        P.tt("dve", W(LRE, n), W(RH, n), W(CS, n), ALU.mult, [b[RH], b[CS]], [b[LRE]])
        P.tt("dve", W(LIM, n), W(RH, n), W(SN, n), ALU.mult, [b[RH], b[SN]], [b[LIM]])
        P.ts("dve", W(NR, n), W(LRE, n), -1.0, None, ALU.add, None, [b[LRE]], [b[NR]])
        P.tt("dve", W(DEN, n), are, are, ALU.mult, rd, [b[DEN]])
        P.tt("dve", W(T, n), aim, aim, ALU.mult, rd, [b[T]])
        P.tt("dve", W(DEN, n), W(DEN, n), W(T, n), ALU.add, [b[DEN], b[T]], [b[DEN]])
        P.op("dve", lambda e: e.reciprocal(out=W(DEN, n), in_=W(DEN, n)), reads=[b[DEN]], writes=[b[DEN]])
        P.tt("dve", W(CRE, n), W(NR, n), are, ALU.mult, [b[NR]] + rd, [b[CRE]])
        P.tt("dve", W(T, n), W(LIM, n), aim, ALU.mult, [b[LIM]] + rd, [b[T]])
        P.tt("dve", W(CRE, n), W(CRE, n), W(T, n), ALU.add, [b[CRE], b[T]], [b[CRE]])
        P.tt("dve", W(CRE, n), W(CRE, n), W(DEN, n), ALU.mult, [b[CRE], b[DEN]], [b[CRE]])
        P.tt("dve", W(CIM, n), W(LIM, n), are, ALU.mult, [b[LIM]] + rd, [b[CIM]])
        P.tt("dve", W(T, n), W(NR, n), aim, ALU.mult, [b[NR]] + rd, [b[T]])
        P.tt("dve", W(CIM, n), W(CIM, n), W(T, n), ALU.subtract, [b[CIM], b[T]], [b[CIM]])
        P.tt("dve", W(CIM, n), W(CIM, n), W(DEN, n), ALU.mult, [b[CIM], b[DEN]], [b[CIM]])
        return CRE, CIM, RH, CS, SN

    for l in range(NL):
        raw = ld_sem[3][:, l, :]
        b_raw = ld_sem[4][l]
        CRE, CIM, RH, CS, SN = coef(32, raw[:, SN_ARE:SN_ARE + 32], raw[:, SN_AIM:SN_AIM + 32],
                                    raw[:, SN_LDT:SN_LDT + 32], True)
        P.cp("dve", RHO[:, l, :], W(RH, 32), [b_wk[RH]], [b_RHO])
        def bc(i):
            return wk[:, i, 0:32].unsqueeze(2).to_broadcast([128, 32, 16])
        bre = raw[:, SN_BRE:SN_BRE + 512].rearrange("p (g h) -> p g h", g=32)
        bim = raw[:, SN_BIM:SN_BIM + 512].rearrange("p (g h) -> p g h", g=32)
        A1, A2 = 13, 14
        v3 = lambda ap: ap.rearrange("p (g h) -> p g h", g=32)
        P.tt("dve", v3(W(A1, 512)), bre, bc(CRE), ALU.mult, [b_raw, b_wk[CRE]], [b_wk[A1]])
        P.tt("dve", v3(W(15, 512)), bim, bc(CIM), ALU.mult, [b_raw, b_wk[CIM]], [b_wk[15]])
        P.tt("dve", W(A1, 512), W(A1, 512), W(15, 512), ALU.subtract, [b_wk[A1], b_wk[15]], [b_wk[A1]])
        P.tt("dve", v3(W(A2, 512)), bim, bc(CRE), ALU.mult, [b_raw, b_wk[CRE]], [b_wk[A2]])
        P.tt("dve", v3(W(15, 512)), bre, bc(CIM), ALU.mult, [b_raw, b_wk[CIM]], [b_wk[15]])
        P.tt("dve", W(A2, 512), W(A2, 512), W(15, 512), ALU.add, [b_wk[A2], b_wk[15]], [b_wk[A2]])
        P.cp("dve", bbN[0:64, 0, :], wk[0:64, A1, :], [b_wk[A1]], [b_bbN])
        P.cp("dve", bbN[64:128, 0, :], wk[64:128, A2, :], [b_wk[A2]], [b_bbN])
        cre_ = raw[:, SN_CRE:SN_CRE + 512]
        cim_ = raw[:, SN_CIM:SN_CIM + 512]
        P.cp("dve", ccN[0:64, 0, :], cre_[0:64, :], [b_raw], [b_ccN])
        P.ts("dve", ccN[64:128, 0, :], cim_[64:128, :], -1.0, None, ALU.mult, None, [b_raw], [b_ccN])
        P.ts("dve", ccN[0:64, 1, :], cim_[0:64, :], -1.0, None, ALU.mult, None, [b_raw], [b_ccN])
        P.ts("dve", ccN[64:128, 1, :], cre_[64:128, :], -1.0, None, ALU.mult, None, [b_raw], [b_ccN])
        for j in range(4):
            dcol = cst[:, l * CL + C_SSMD + j:l * CL + C_SSMD + j + 1]
            P.ts("dve", Kdir[:, l, j, :], cst[:, C_IDENT:C_IDENT + 128], dcol, None, ALU.mult, None, [b_cst], [b_Kdir])
        for gc in range(4):
            gs = slice(8 * gc, 8 * gc + 8)
            par = gc % 2
            Ct, b_Ct = Ct_l[par]
            St, b_St = St_l[par]
            P.cp("dve", Ct[:, :, 0:1], wk[:, CS, gs].unsqueeze(2), [b_wk[CS]], [b_Ct])
            P.cp("dve", St[:, :, 0:1], wk[:, SN, gs].unsqueeze(2), [b_wk[SN]], [b_St])
            m = 1
            while m < 512:
                cm = Ct[:, :, m - 1:m].to_broadcast([128, 8, m])
                sm = St[:, :, m - 1:m].to_broadcast([128, 8, m])
                P.tt("dve", tm1[:, :, 0:m], St[:, :, 0:m], sm, ALU.mult, [b_St], [b_tm1])
                P.tt("dve", tm2[:, :, 0:m], Ct[:, :, 0:m], sm, ALU.mult, [b_Ct, b_St], [b_tm2])
                P.tt("dve", Ct[:, :, m:2 * m], Ct[:, :, 0:m], cm, ALU.mult, [b_Ct], [b_Ct])
                P.tt("dve", St[:, :, m:2 * m], St[:, :, 0:m], cm, ALU.mult, [b_St, b_Ct], [b_St])
                P.tt("dve", Ct[:, :, m:2 * m], Ct[:, :, m:2 * m], tm1[:, :, 0:m], ALU.subtract, [b_Ct, b_tm1], [b_Ct])
                P.tt("dve", St[:, :, m:2 * m], St[:, :, m:2 * m], tm2[:, :, 0:m], ALU.add, [b_St, b_tm2], [b_St])
                m *= 2
            COSL, SINL, b_COSL = ld_sem[0:3]
            P.cp("dve", COSL[:, l, gs].unsqueeze(2), Ct[:, :, 511:512], [b_Ct], [b_COSL])
            P.cp("dve", SINL[:, l, gs].unsqueeze(2), St[:, :, 511:512], [b_St], [b_COSL])
            P.dma("sp", sst[l, 8 * gc:8 * gc + 8, :, 0:512].rearrange("g q k -> q g k"), Ct[:, :, :],
                  [b_Ct], [b_ssm[l][0 + par]], ssm_sem[0] if par == 0 else ssm_sem[4])
            P.dma("sp", sst[l, 8 * gc:8 * gc + 8, :, 512:1024].rearrange("g q k -> q g k"), St[:, :, :],
                  [b_St], [b_ssm[l][2 + par]], ssm_sem[1] if par == 0 else ssm_sem[5])
        CRE, CIM, RH, CS, SN = coef(512, raw[:, ST_ARE:ST_ARE + 512], raw[:, ST_AIM:ST_AIM + 512],
                                    raw[:, ST_LDT:ST_LDT + 512], False)
        btre = raw[:, ST_BRE:ST_BRE + 512]
        btim = raw[:, ST_BIM:ST_BIM + 512]
        P.tt("dve", W(A1, 512), btre, W(CRE, 512), ALU.mult, [b_raw, b_wk[CRE]], [b_wk[A1]])
        P.tt("dve", W(15, 512), btim, W(CIM, 512), ALU.mult, [b_raw, b_wk[CIM]], [b_wk[15]])
        P.tt("dve", W(A1, 512), W(A1, 512), W(15, 512), ALU.subtract, [b_wk[A1], b_wk[15]], [b_wk[A1]])
        P.tt("dve", W(A2, 512), btim, W(CRE, 512), ALU.mult, [b_raw, b_wk[CRE]], [b_wk[A2]])
        P.tt("dve", W(15, 512), btre, W(CIM, 512), ALU.mult, [b_raw, b_wk[CIM]], [b_wk[15]])
        P.tt("dve", W(A2, 512), W(A2, 512), W(15, 512), ALU.add, [b_wk[A2], b_wk[15]], [b_wk[A2]])
        v4 = lambda ap: ap.rearrange("p (j q) -> p j q", j=4)
        P.cp("dve", v4(bbT[:, 0, :])[:, :, 0:64], v4(W(A1, 512))[:, :, 0:64], [b_wk[A1]], [b_bbT])
        P.cp("dve", v4(bbT[:, 0, :])[:, :, 64:128], v4(W(A2, 512))[:, :, 64:128], [b_wk[A2]], [b_bbT])
        P.cp("dve", v4(bbT[:, 1, :])[:, :, 0:64], v4(W(A2, 512))[:, :, 0:64], [b_wk[A2]], [b_bbT])
        P.ts("dve", v4(bbT[:, 1, :])[:, :, 64:128], v4(W(A1, 512))[:, :, 64:128], -1.0, None, ALU.mult, None,
             [b_wk[A1]], [b_bbT])
        for j in range(4):
            par = j % 2
            wst, b_wst = wst_l[par]
            P.memset("dve", wst[:], 0.0, [b_wst])
            for gp in range(8):
                g = 8 * j + gp
                mcol = praw[:, PR_M8 + gp:PR_M8 + gp + 1]
                P.ts("dve", wst[:, gp, 0:128], bbT[:, 0, j * 128:(j + 1) * 128], mcol, None, ALU.mult, None,
                     [b_bbT, b_praw], [b_wst])
                P.ts("dve", wst[:, gp, 128:256], bbT[:, 1, j * 128:(j + 1) * 128], mcol, None, ALU.mult, None,
                     [b_bbT, b_praw], [b_wst])
                P.cp("dve", wst[:, gp, 256 + 16 * gp:256 + 16 * gp + 16], ccN[:, 0, g * 16:(g + 1) * 16], [b_ccN], [b_wst])
                P.cp("dve", wst[:, gp, 384 + 16 * gp:384 + 16 * gp + 16], ccN[:, 1, g * 16:(g + 1) * 16], [b_ccN], [b_wst])
            P.dma("sp", ssw[l, 8 * j:8 * j + 8, :, :].rearrange("g q n -> q g n"), wst[:, :, :],
                  [b_wst], [b_ssm[l][4 + par]], ssm_sem[2] if par == 0 else ssm_sem[6])


def _km(cols):
    return cols.reshape(8, 128, -1).transpose(1, 0, 2)


def pack_weights(inp):
    out = np.empty((2, 128, NW), np.float32)
    for l in range(2):
        w_in = np.asarray(inp["w_in"][l], np.float32)
        parts = []
        A = w_in[:, 0:1536]
        for j in range(4):
            blk = np.stack([_km(A[:, s * 512 + j * 128:s * 512 + (j + 1) * 128]) for s in range(3)], axis=2)
            parts.append(blk.reshape(128, -1))
        for c0 in (1536, 2048, 2560, 3072, 3584):
            parts.append(_km(w_in[:, c0:c0 + 512]).reshape(128, -1))
        parts.insert(0, _km(w_in[:, 4096:4608]).reshape(128, -1))
        parts.append(np.asarray(inp["w_glu"][l]).reshape(4, 128, 512).transpose(1, 0, 2).reshape(128, -1))
        G = w_in[:, 4608:8704]
        w_br = np.asarray(inp["w_br"][l])
        for m in range(8):
            gm = np.stack([_km(G[:, b * 1024 + m * 128:b * 1024 + (m + 1) * 128]) for b in range(4)], axis=2)
            parts.append(gm.reshape(128, -1))
            br = w_br[:, :, m * 128:(m + 1) * 128].reshape(4, 4, 128, 128).transpose(2, 0, 1, 3)
            parts.append(br.reshape(128, -1))
        wo = np.asarray(inp["w_o"][l])
        for half in range(2):
            parts.append(_km(wo[:, half * 512:(half + 1) * 512]).reshape(128, -1))
        wg = np.asarray(inp["w_ffn_gate"][l])
        wu = np.asarray(inp["w_ffn_up"][l])
        for fp in range(11):
            blk = np.stack([np.stack([_km(wg[:, f * 128:(f + 1) * 128]), _km(wu[:, f * 128:(f + 1) * 128])], axis=1)
                            for f in (2 * fp, 2 * fp + 1)], axis=1)
            parts.append(blk.reshape(128, -1))
        wd = np.asarray(inp["w_ffn_down"][l])
        for m in range(8):
            parts.append(wd[:, m * 128:(m + 1) * 128].reshape(22, 128, 128).transpose(1, 0, 2).reshape(128, -1))
        cat = np.concatenate(parts, axis=1)
        assert cat.shape == (128, NW), cat.shape
        out[l] = cat
    return out


def pack_small(inp):
    f = lambda a: np.asarray(a, np.float32)
    cst = np.zeros((128, NCST), np.float32)
    ltab = np.zeros((2, 128, 1536), np.float32)
    praw = np.zeros((128, NPR), np.float32)
    ssmraw = np.zeros((2, 128, NSR), np.float32)
    for l in range(2):
        b = l * CL
        cst[:, b + C_GMIX:b + C_GMIX + 8] = f(inp["g_mix"][l]).reshape(8, 128).T
        cst[:, b + C_GFFN:b + C_GFFN + 8] = f(inp["g_ffn"][l]).reshape(8, 128).T
        cw = f(inp["conv_w"][l])
        for k in range(3):
            cst[:, b + C_CONVW + 4 * k:b + C_CONVW + 4 * k + 4] = cw[k].reshape(4, 128).T
        cst[:, b + C_CONVB:b + C_CONVB + 4] = f(inp["conv_b"][l]).reshape(4, 128).T
        cst[:, b + C_BGLU:b + C_BGLU + 4] = f(inp["b_glu"][l]).reshape(4, 128).T
        cst[:, b + C_SSMD:b + C_SSMD + 4] = f(inp["ssm_d"][l]).reshape(4, 128).T
        cst[:, b + C_SUBLN] = f(inp["subln_g"][l])
        ltab[l, :, 0:512] = f(inp["sg_ln_g"][l])[None, :]
        ltab[l, :, 512:1024] = f(inp["sg_ln_b"][l])[None, :]
        ltab[l, :, 1024:1536] = f(inp["sg_b"][l]).reshape(1, 512)
        praw[:, l * PR_L:l * PR_L + 512] = f(inp["sg_w"][l]).transpose(2, 0, 1).reshape(128, 512)
        praw[:, l * PR_L + 512:l * PR_L + 768] = f(inp["lam_qk"][l]).reshape(1, 256)
        a_re = f(inp["ssm_a_re"][l]); a_im = f(inp["ssm_a_im"][l]); ldt = f(inp["ssm_log_dt"][l])
        b_re = f(inp["ssm_b_re"][l]); b_im = f(inp["ssm_b_im"][l])
        c_re = f(inp["ssm_c_re"][l]); c_im = f(inp["ssm_c_im"][l])
        sr = ssmraw[l]
        sr[:, SN_ARE:SN_ARE + 32] = np.tile(a_re.T, (2, 1))
        sr[:, SN_AIM:SN_AIM + 32] = np.tile(a_im.T, (2, 1))
        sr[:, SN_LDT:SN_LDT + 32] = ldt[None, :]
        sr[:, SN_BRE:SN_BRE + 512] = np.tile(b_re.transpose(1, 0, 2), (2, 1, 1)).reshape(128, 512)
        sr[:, SN_BIM:SN_BIM + 512] = np.tile(b_im.transpose(1, 0, 2), (2, 1, 1)).reshape(128, 512)
        sr[:, SN_CRE:SN_CRE + 512] = np.tile(c_re.transpose(2, 0, 1), (2, 1, 1)).reshape(128, 512)
        sr[:, SN_CIM:SN_CIM + 512] = np.tile(c_im.transpose(2, 0, 1), (2, 1, 1)).reshape(128, 512)

        def tl(a):
            t = np.tile(a.reshape(4, 8, 64), (1, 1, 2)).transpose(1, 0, 2)
            return np.repeat(t[:, None], 16, axis=1).reshape(128, 512)
        sr[:, ST_ARE:ST_ARE + 512] = tl(a_re)
        sr[:, ST_AIM:ST_AIM + 512] = tl(a_im)
        sr[:, ST_LDT:ST_LDT + 512] = tl(np.repeat(ldt[:, None], 64, axis=1))

        def tb(bb):
            t = bb.reshape(4, 8, 64, 16).transpose(1, 3, 0, 2)
            return np.tile(t, (1, 1, 1, 2)).reshape(128, 512)
        sr[:, ST_BRE:ST_BRE + 512] = tb(b_re)
        sr[:, ST_BIM:ST_BIM + 512] = tb(b_im)
    cst[:, C_GFINAL:C_GFINAL + 8] = f(inp["g_final"]).reshape(8, 128).T
    cst[:, C_IDENT:C_IDENT + 128] = np.eye(128, dtype=np.float32)
    J = np.zeros((128, 128), np.float32)
    for q in range(64):
        J[q + 64, q] = -1.0
        J[q, q + 64] = 1.0
    cst[:, C_JMAT:C_JMAT + 128] = J
    s_ = np.arange(128)
    praw[:, PR_TRIL:PR_TRIL + 128] = (s_[:, None] <= s_[None, :]).astype(np.float32)
    praw[:, PR_BD:PR_BD + 128] = (s_[:, None] // 16 == s_[None, :] // 16).astype(np.float32)
    praw[:, PR_M8:PR_M8 + 8] = (s_[:, None] // 16 == np.arange(8)[None, :]).astype(np.float32)
    return cst, ltab, praw, ssmraw


_CACHE = {}


def kernel(**inputs):
    x = np.asarray(inputs["x"], np.float32)
    wpk = pack_weights(inputs)
    cst, ltab, praw, ssmraw = pack_small(inputs)
    if "nc" not in _CACHE:
        _CACHE["nc"] = build_program()
    nc = _CACHE["nc"]
    in_maps = []
    for b in range(8):
        in_maps.append({"xT": np.ascontiguousarray(x[b].T), "wpk": wpk, "cst": cst, "ltab": ltab,
                        "praw": praw, "ssmraw": ssmraw})
    res = run_bass_kernel_spmd(nc, in_maps, core_ids=list(range(8)))
    out = np.stack([np.ascontiguousarray(res.results[b]["outT"].T) for b in range(8)], axis=0)
    return out.astype(np.float32)
```

```python
import math
from contextlib import ExitStack

import numpy as np
import concourse.bass as bass
import concourse.mybir as mybir
from concourse.bass_utils import run_bass_kernel_spmd

F32 = mybir.dt.float32
BF16 = mybir.dt.bfloat16
AF = mybir.ActivationFunctionType
ALU = mybir.AluOpType
AX = mybir.AxisListType

TT = 512
SEQ = 4096
DM = 1024
NSLOT = 4
SLOT = 4096
NSS = 3
MAGIC = 12582912.0
PI = math.pi

REC = []
REC.append(("Du", 4096))
for _j in range(4):
    REC.append(("A%d" % _j, 3072))
for _n in ("Bu", "Bv", "Cq", "Ck", "Cv"):
    REC.append((_n, 4096))
REC.append(("Wglu", 2048))
for _m in range(8):
    REC.append(("G%d" % _m, 4096))
    REC.append(("BR%d" % _m, 2048))
REC.append(("WO0", 4096))
REC.append(("WO1", 4096))
for _f in range(11):
    REC.append(("F%d" % _f, 4096))
for _m in range(8):
    REC.append(("D%d" % _m, 2816))
REC_OFF = {}
_o = 0
for _n, _s in REC:
    REC_OFF[_n] = (_o, _s)
    _o += _s
NW = _o

CL = 48
C_GMIX, C_GFFN, C_CONVW, C_CONVB, C_BGLU, C_SSMD, C_SUBLN = 0, 8, 16, 28, 32, 36, 40
C_GFINAL = 96
C_IDENT = 104
C_JMAT = 232
NCST = 360
PR_L = 768
PR_TRIL = 1536
PR_BD = 1664
PR_M8 = 1792
NPR = 1800
SN_ARE, SN_AIM, SN_LDT, SN_BRE, SN_BIM, SN_CRE, SN_CIM = 0, 32, 64, 96, 608, 1120, 1632
ST_ARE, ST_AIM, ST_LDT, ST_BRE, ST_BIM = 2144, 2656, 3168, 3680, 4192
NSR = 4704


class Eng:
    def __init__(self, name, sem):
        self.name = name
        self.sem = sem
        self.cnt = 0
        self.ops = []
        self.seen = {}


class DmaSem:
    def __init__(self, sem):
        self.sem = sem
        self.cnt = 0


class Buf:
    __slots__ = ("name", "w", "r")

    def __init__(self, name=""):
        self.name = name
        self.w = None
        self.r = {}


class Pool_:
    def __init__(self, items):
        self.free = list(items)

    def get(self):
        assert self.free, "pool exhausted"
        return self.free.pop(0)

    def put(self, it):
        self.free.append(it)


class Prog:
    def __init__(self, nc, stack):
        self.nc = nc
        self.stack = stack
        self.eng = {}
        for n in ("pe", "act", "dve", "pool", "sp"):
            sem = stack.enter_context(nc.semaphore("sem_" + n))
            self.eng[n] = Eng(n, sem)
        self.pe_sems = {id(self.eng["pe"].sem)}

    def dma_sem(self, name):
        return DmaSem(self.stack.enter_context(self.nc.semaphore(name)))

    def new_epoch(self, tag):
        for n, E in self.eng.items():
            E.sem = self.stack.enter_context(self.nc.semaphore("sem_%s_%s" % (n, tag)))
            E.cnt = 0
            if n == "pe":
                self.pe_sems.add(id(E.sem))

    def op(self, eng, fn, reads=(), writes=(), inc=True, dma=None):
        E = self.eng[eng]
        waits = {}

        def need(c):
            if c is None:
                return
            sem, val = c
            if E.name == "pe" and id(sem) in self.pe_sems:
                return
            k = id(sem)
            if E.seen.get(k, 0) >= val:
                return
            if k not in waits or waits[k][1] < val:
                waits[k] = (sem, val)

        for b in reads:
            need(b.w)
        for b in writes:
            need(b.w)
            for c in b.r.values():
                need(c)
        for k, (sem, val) in waits.items():
            E.seen[k] = val
        if dma is not None:
            dma.cnt += 16
            comp = (dma.sem, dma.cnt)
            incspec = (dma.sem, 16)
        elif inc:
            E.cnt += 1
            comp = (E.sem, E.cnt)
            incspec = (E.sem, 1)
        else:
            comp = (E.sem, E.cnt + 1)
            incspec = None
        ck = id(comp[0])
        for b in reads:
            if ck not in b.r or b.r[ck][1] < comp[1]:
                b.r[ck] = comp
        for b in writes:
            b.w = comp
            b.r = {}
        E.ops.append((list(waits.values()), fn, incspec))
        return comp

    def wait_all(self, eng, comps):
        E = self.eng[eng]
        ws = []
        for sem, val in comps:
            if E.seen.get(id(sem), 0) < val:
                E.seen[id(sem)] = val
                ws.append((sem, val))
        E.ops.append((ws, None, None))

    def barrier(self, dma_sems=()):
        comps = [(E.sem, E.cnt) for E in self.eng.values() if E.cnt > 0]
        comps += [(d.sem, d.cnt) for d in dma_sems if d.cnt > 0]
        for n in self.eng:
            self.wait_all(n, comps)

    def replay(self):
        nc = self.nc
        with nc.Block() as block:
            def mk(E):
                ops = E.ops

                def body(e):
                    for waits, fn, incspec in ops:
                        for sem, val in waits:
                            e.wait_ge(sem, val)
                        if fn is None:
                            continue
                        ins = fn(e)
                        if incspec is not None:
                            ins.then_inc(incspec[0], incspec[1])
                return body
            block.tensor(mk(self.eng["pe"]))
            block.scalar(mk(self.eng["act"]))
            block.vector(mk(self.eng["dve"]))
            block.gpsimd(mk(self.eng["pool"]))
            block.sync(mk(self.eng["sp"]))
        for E in self.eng.values():
            E.ops = []

    def mm(self, out, lhsT, rhs, start, stop, reads, writes, inc):
        self.op("pe", lambda e: e.matmul(out, lhsT=lhsT, rhs=rhs, start=start, stop=stop),
                reads=reads, writes=writes, inc=True)

    def act(self, out, in_, func, reads, writes, bias=None, scale=None):
        kw = {}
        if bias is not None:
            kw["bias"] = bias
        if scale is not None:
            kw["scale"] = scale
        self.op("act", lambda e: e.activation(out=out, in_=in_, func=func, **kw), reads=reads, writes=writes)

    def tt(self, eng, out, in0, in1, op, reads, writes):
        self.op(eng, lambda e: e.tensor_tensor(out=out, in0=in0, in1=in1, op=op), reads=reads, writes=writes)

    def ts(self, eng, out, in0, s1, s2, op0, op1, reads, writes):
        if op1 is None:
            self.op(eng, lambda e: e.tensor_scalar(out=out, in0=in0, scalar1=s1, scalar2=None, op0=op0),
                    reads=reads, writes=writes)
        else:
            self.op(eng, lambda e: e.tensor_scalar(out=out, in0=in0, scalar1=s1, scalar2=s2, op0=op0, op1=op1),
                    reads=reads, writes=writes)

    def stt(self, eng, out, in0, scalar, in1, op0, op1, reads, writes):
        self.op(eng, lambda e: e.scalar_tensor_tensor(out=out, in0=in0, scalar=scalar, in1=in1, op0=op0, op1=op1),
                reads=reads, writes=writes)

    def cp(self, eng, out, in_, reads, writes):
        if eng == "act":
            self.op("act", lambda e: e.activation(out=out, in_=in_, func=AF.Copy), reads=reads, writes=writes)
        else:
            self.op(eng, lambda e: e.tensor_copy(out=out, in_=in_), reads=reads, writes=writes)

    def memset(self, eng, ap, val, writes):
        self.op(eng, lambda e: e.memset(ap, val), writes=writes)

    def dma(self, eng, out, in_, reads, writes, sem):
        self.op(eng, lambda e: e.dma_start(out=out, in_=in_), reads=reads, writes=writes, dma=sem)


def build_program(NT=8, NL=2, branches="ABCD", do_ffn=True):
    nc = bass.Bass("TRN2", target_bir_lowering=False)
    xT_in = nc.dram_tensor("xT", [DM, SEQ], F32, kind="ExternalInput").ap()
    wpk = nc.dram_tensor("wpk", [2, 128, NW], F32, kind="ExternalInput").ap()
    cst_in = nc.dram_tensor("cst", [128, NCST], F32, kind="ExternalInput").ap()
    ltab_in = nc.dram_tensor("ltab", [2, 128, 1536], F32, kind="ExternalInput").ap()
    praw_in = nc.dram_tensor("praw", [128, NPR], F32, kind="ExternalInput").ap()
    ssmraw_in = nc.dram_tensor("ssmraw", [2, 128, NSR], F32, kind="ExternalInput").ap()
    outT = nc.dram_tensor("outT", [DM, SEQ], F32, kind="ExternalOutput").ap()
    wb = nc.dram_tensor("wb", [2, 128, NW], BF16, kind="Internal").ap()
    xs = nc.dram_tensor("xs", [DM, SEQ], F32, kind="Internal").ap()
    ssw = nc.dram_tensor("ssw", [2, 32, 128, 512], BF16, kind="Internal").ap()
    sst = nc.dram_tensor("sst", [2, 32, 128, 1024], F32, kind="Internal").ap()

    with ExitStack() as st:
        P = Prog(nc, st)

        def sbuf(name, shape, dt, stack=st):
            return stack.enter_context(nc.sbuf_tensor(name, list(shape), dt))

        cst = sbuf("cst_sb", [128, NCST], F32); b_cst = Buf()
        ones_bf = sbuf("ones_bf", [128, 128], BF16); b_ones = Buf()
        WsT = sbuf("WsT", [128, 2, 4, 128], BF16); b_WsT = Buf()
        Kdir = sbuf("Kdir", [128, 2, 4, 128], BF16); b_Kdir = Buf()
        RHO = sbuf("RHO", [128, 2, 32], F32); b_RHO = Buf()
        COSL = sbuf("COSL", [128, 2, 32], F32); b_COSL = Buf()
        SINL = sbuf("SINL", [128, 2, 32], F32)
        neglam = sbuf("neglam", [128, 2], F32); b_neglam = Buf()
        banks = []
        for i in range(8):
            ps = st.enter_context(nc.psum_tensor("ps%d" % i, [128, 512], F32))
            banks.append((ps, Buf("bank%d" % i)))
        PS = Pool_(banks)

        cst_sem = P.dma_sem("ld_cst")
        praw_sem = P.dma_sem("ld_praw")
        raw_sem = P.dma_sem("ld_raw")
        ct_sem = P.dma_sem("st_ct")
        st_sem = P.dma_sem("st_st")
        wst_sem = P.dma_sem("st_wst")
        ct_sem2 = P.dma_sem("st_ct2")
        st_sem2 = P.dma_sem("st_st2")
        wst_sem2 = P.dma_sem("st_wst2")
        b_ssm = [[Buf() for _ in range(6)] for _ in range(2)]
        ssm_sem = (ct_sem, st_sem, wst_sem, raw_sem, ct_sem2, st_sem2, wst_sem2)

        P.dma("sp", cst[:], cst_in[:, :], [], [b_cst], cst_sem)
        P.memset("dve", ones_bf[:], 1.0, [b_ones])

        with ExitStack() as pst:
            praw = sbuf("praw_sb", [128, NPR], F32, pst); b_praw = Buf()
            P.dma("sp", praw[:], praw_in[:, :], [], [b_praw], praw_sem)
            raw2 = sbuf("ssm_raw2", [128, 2, NSR], F32, pst)
            b_raw2 = [Buf(), Buf()]
            raw_sems = [raw_sem, P.dma_sem("ld_raw1")]
            if "D" in branches:
                for l in range(NL):
                    P.dma("sp", raw2[:, l, :], ssmraw_in[l, :, :], [], [b_raw2[l]], raw_sems[l])
            P.wait_all("pool", [(d_.sem, d_.cnt) for d_ in (cst_sem, praw_sem, raw_sems[0], raw_sems[1]) if d_.cnt > 0])
            CH = 8192
            NCH = NW // CH
            b_wbc = [[Buf() for _ in range(NCH)] for _ in range(2)]
            for l in range(1):
                for ci in range(NCH):
                    c0 = ci * CH
                    src = wpk[l, :, c0:c0 + CH].rearrange("p (a b) -> p a b", b=2048)
                    dst = wb[l, :, c0:c0 + CH].rearrange("p (a b) -> p a b", b=2048)
                    P.dma("pool", dst, src, [], [b_wbc[l][ci]], P.dma_sem("cast%d_%d" % (l, ci)))

            for l in range(NL):
                P.tt("dve", WsT[:, l, :, :],
                     praw[:, l * PR_L:l * PR_L + 512].rearrange("p (g t) -> p g t", g=4),
                     praw[:, PR_TRIL:PR_TRIL + 128].unsqueeze(1).to_broadcast([128, 4, 128]),
                     ALU.mult, [b_praw], [b_WsT])
            lt = sbuf("lam_t", [128, 8], F32, pst); b_lt = Buf()
            lp = sbuf("lam_p", [128, 128], F32, pst); b_lp = Buf()
            for l in range(NL):
                lam_init = 0.8 - 0.6 * math.exp(-0.3 * l)
                lq = praw[:, l * PR_L + 512:l * PR_L + 768]
                P.tt("dve", lp[:, 0:64], lq[:, 0:64], lq[:, 64:128], ALU.mult, [b_praw], [b_lp])
                P.tt("dve", lp[:, 64:128], lq[:, 128:192], lq[:, 192:256], ALU.mult, [b_praw], [b_lp])
                P.op("dve", lambda e: e.tensor_reduce(out=lt[:, 0:2], in_=lp[:, :].rearrange("p (a b) -> p a b", a=2),
                                                      axis=AX.X, op=ALU.add), reads=[b_lp], writes=[b_lt])
                P.act(lt[:, 2:4], lt[:, 0:2], AF.Exp, [b_lt], [b_lt])
                P.tt("dve", lt[:, 4:5], lt[:, 3:4], lt[:, 2:3], ALU.subtract, [b_lt], [b_lt])
                P.ts("dve", neglam[:, l:l + 1], lt[:, 4:5], -lam_init, None, ALU.add, None, [b_lt], [b_neglam])

            if "D" in branches:
                _ssm_prologue(nc, P, pst, sbuf, NL, ssmraw_in, praw, b_praw, cst, b_cst, ssw, sst, ssm_sem, b_ssm,
                              Kdir, b_Kdir, RHO, b_RHO, PS, (COSL, SINL, b_COSL, raw2, b_raw2))
            P.barrier([cst_sem, praw_sem, raw_sems[0], raw_sems[1], ct_sem, st_sem, wst_sem, ct_sem2, st_sem2, wst_sem2])
            P.replay()

        for l in range(1, NL):
            for ci in range(NCH):
                c0 = ci * CH
                src = wpk[l, :, c0:c0 + CH].rearrange("p (a b) -> p a b", b=2048)
                dst = wb[l, :, c0:c0 + CH].rearrange("p (a b) -> p a b", b=2048)
                P.dma("pool", dst, src, [], [b_wbc[l][ci]], P.dma_sem("cast%d_%d" % (l, ci)))
        ring = sbuf("ring", [128, NSLOT, SLOT], BF16)
        ring_b = [Buf("ring%d" % i) for i in range(NSLOT)]
        ring_sem = [P.dma_sem("ring%d" % i) for i in range(NSLOT)]
        sr_bb = sbuf("sr_bb", [128, NSS, 256], BF16)
        sr_cc = sbuf("sr_cc", [128, NSS, 256], BF16)
        sr_tab = sbuf("sr_tab", [128, NSS, 1024], F32)
        KT = sbuf("KT", [128, 4, SEQ], BF16)
        b_KT = [[Buf() for _ in range(8)] for _ in range(4)]
        Vc = sbuf("Vc", [128, 32, 512], BF16)
        b_Vc = [Buf() for _ in range(32)]
        x = sbuf("x", [128, 8, TT], F32)
        b_x = [Buf("x%d" % c) for c in range(8)]
        hT = sbuf("hT", [128, 8, TT], BF16)
        b_h = [Buf("h%d" % c) for c in range(8)]
        arena = sbuf("arena", [128, 28, TT], BF16)
        b_ar = [Buf("ar%d" % c) for c in range(28)]
        qpad = sbuf("qpad", [128, 4, 2, 2, 256], BF16)
        b_qp = [Buf("qp%d" % h) for h in range(4)]
        P.memset("pool", qpad[:], 0.0, b_qp)
        NTF = 8
        tfp = sbuf("tfp", [128, NTF, 514], F32)
        TF = Pool_([(tfp[:, i, :], Buf("tf%d" % i)) for i in range(NTF)])
        NTB = 7
        tbp = sbuf("tbp", [128, NTB, TT], BF16)
        TB = Pool_([(tbp[:, i, :], Buf("tb%d" % i)) for i in range(NTB)])
        ltab = sbuf("ltab_sb", [128, 1536], F32); b_ltab = Buf()
        zcarry = sbuf("zcarry", [128, 4, 2], F32); b_zc = Buf()
        scarry = sbuf("scarry", [128, 32], F32); b_scarry = Buf()
        zl = sbuf("zl", [128, 32], F32); b_zl = Buf()
        ccb = sbuf("ccb", [128, 64], F32); b_ccb = Buf()
        small = sbuf("small", [128, 16], F32); b_small = Buf()
        xld_c = [P.dma_sem("xld%d" % c) for c in range(8)]
        xst_c = [P.dma_sem("xst%d" % c) for c in range(8)]
        lt_sem = P.dma_sem("ltab")

        seq = []
        for l in range(NL):
            for i in range(NT):
                for name, size in REC:
                    seq.append((l, name))
        def used(name):
            if name.startswith("A"):
                return "A" in branches
            if name.startswith("B") and not name.startswith("BR"):
                return "B" in branches
            if name.startswith("C"):
                return "C" in branches
            if name in ("Du", "Wglu"):
                return "D" in branches
            if name.startswith("F") or (name.startswith("D") and name not in ("Du",)):
                return do_ffn
            return True
        seq = [(l, n) for (l, n) in seq if used(n)]
        state = {"next_load": 0, "next_get": 0, "sload": 0, "sget": 0}

        def ring_get(expect):
            r = state["next_get"]
            assert seq[r][1] == expect, (seq[r], expect)
            while state["next_load"] < len(seq) and state["next_load"] <= r + NSLOT - 2:
                q = state["next_load"]
                l, name = seq[q]
                off, size = REC_OFF[name]
                s = q % NSLOT
                cbs = [b_wbc[l][ci] for ci in range(off // CH, (off + size - 1) // CH + 1)]
                P.dma("sp", ring[:, s, 0:size], wb[l, :, off:off + size], cbs, [ring_b[s]], ring_sem[s])
                state["next_load"] += 1
            state["next_get"] += 1
            s = r % NSLOT
            return ring[:, s, :], ring_b[s]

        sseq = [(l, g) for l in range(NL) for i in range(NT) for g in range(32)] if "D" in branches else []

        class Stream:
            def __init__(self, name, issue):
                self.bufs = [Buf() for _ in range(NSS)]
                self.sems = [P.dma_sem("%s%d" % (name, k)) for k in range(NSS)]
                self.nload = 0
                self.nget = 0
                self.issue = issue

            def get(self):
                r = self.nget
                while self.nload < len(sseq) and self.nload <= r + NSS - 2:
                    q = self.nload
                    self.issue(sseq[q], q % NSS, self.bufs[q % NSS], self.sems[q % NSS])
                    self.nload += 1
                self.nget += 1
                return r % NSS

        st_bb = Stream("srbb", lambda it, s_, b, sem: P.dma("sp", sr_bb[:, s_, :], ssw[it[0], it[1], :, 0:256],
                                                            [b_ssm[it[0]][4], b_ssm[it[0]][5]], [b], sem))
        st_cc = Stream("srcc", lambda it, s_, b, sem: P.dma("sp", sr_cc[:, s_, :], ssw[it[0], it[1], :, 256:512],
                                                            [b_ssm[it[0]][4], b_ssm[it[0]][5]], [b], sem))
        st_tab = Stream("srtab", lambda it, s_, b, sem: P.dma("sp", sr_tab[:, s_, :], sst[it[0], it[1], :, :],
                                                              b_ssm[it[0]][0:4], [b], sem))

        def proj_fm(slot_view_fn, rb, bank, bbuf, hbufs=None):
            for k in range(8):
                P.mm(bank[:, :], slot_view_fn(k), hT[:, k, :], k == 0, k == 7,
                     [rb, b_h[k]], [bbuf], k == 7)

        def rmsnorm_stats(eps, count, scale2=1.0):
            bank, bb = PS.get()
            sqs = []
            for c in range(8):
                sq = TB.get()
                P.act(sq[0], x[:, c, :], AF.Square, [b_x[c]], [sq[1]])
                P.mm(bank[:, :], ones_bf[:, :], sq[0], c == 0, c == 7, [b_ones, sq[1]], [bb], c == 7)
                TB.put(sq)
            rt = TF.get()
            P.act(rt[0][:, 0:TT], bank[:, :], AF.Ln, [bb], [rt[1]], bias=eps / scale2, scale=1.0 / (count * scale2))
            PS.put((bank, bb))
            P.act(rt[0][:, 0:TT], rt[0][:, 0:TT], AF.Exp, [rt[1]], [rt[1]], scale=-0.5)
            return rt

        def norm_to_h(gcol):
            rt = rmsnorm_stats(1e-6, 1024.0)
            for c in range(8):
                eng = "dve"
                P.stt(eng, hT[:, c, :], x[:, c, :], cst[:, gcol + c:gcol + c + 1], rt[0][:, 0:TT], ALU.mult, ALU.mult,
                      [b_x[c], b_cst, rt[1]], [b_h[c]])
            TF.put(rt)

        for l in range(NL):
            lam_init = 0.8 - 0.6 * math.exp(-0.3 * l)
            cb = l * CL
            P.new_epoch("L%d" % l)
            P.dma("sp", ltab[:], ltab_in[l, :, :], [], [b_ltab], lt_sem)
            P.memset("dve", zcarry[:], 0.0, [b_zc])
            P.memset("dve", scarry[:], 0.0, [b_scarry])
            for i in range(NT):
                t0 = i * TT
                src = (xT_in if l == 0 else xs).rearrange("(c p) t -> p c t", p=128)[:, :, t0:t0 + TT]
                for c in range(8):
                    P.dma("sp", x[:, c, :], src[:, c, :], [], [b_x[c]], xld_c[c])
                norm_to_h(cb + C_GMIX)

                do_D = "D" in branches
                if do_D:
                    slot, rb = ring_get("Du")
                    sv = slot[:, 0:4096].rearrange("p (k n) -> p k n", k=8)
                    for j in range(4):
                        bk = PS.get()
                        proj_fm(lambda k, j=j: sv[:, k, j * 128:(j + 1) * 128], rb, bk[0], bk[1])
                        P.cp("act" if j % 2 else "dve", arena[:, 16 + j, :], bk[0][:, :], [bk[1]], [b_ar[16 + j]])
                        PS.put(bk)
                else:
                    for j in range(4):
                        P.memset("pool", arena[:, 12 + j, :], 0.0, [b_ar[12 + j]])

                def abc_units(l=l, i=i, cb=cb, t0=t0):
                    for j in range(4):
                        if "A" not in branches:
                            P.memset("pool", arena[:, 0 + j, :], 0.0, [b_ar[0 + j]])
                            continue
                        slot, rb = ring_get("A%d" % j)
                        sv = slot[:, 0:3072].rearrange("p (k s n) -> p k s n", k=8, s=3)
                        pb = PS.get(); pc = PS.get(); px = PS.get()
                        for s_, bk in ((0, pb), (1, pc), (2, px)):
                            proj_fm(lambda k, s_=s_: sv[:, k, s_, :], rb, bk[0], bk[1])
                        axs = TF.get(); z = TF.get(); pbs = TF.get()
                        P.cp("act", axs[0][:, 0:TT], px[0][:, :], [px[1]], [axs[1]])
                        PS.put(px)
                        P.cp("act", pbs[0][:, 0:TT], pb[0][:, :], [pb[1]], [pbs[1]])
                        PS.put(pb)
                        P.tt("dve", z[0][:, 2:514], pc[0][:, :], axs[0][:, 0:TT], ALU.mult, [pc[1], axs[1]], [z[1]])
                        PS.put(pc)
                        acc = axs
                        P.cp("dve", z[0][:, 0:2], zcarry[:, j, :], [b_zc], [z[1]])
                        cw = cb + C_CONVW
                        P.ts("dve", acc[0][:, 0:TT], z[0][:, 2:514], cst[:, cw + 8 + j:cw + 9 + j],
                             cst[:, cb + C_CONVB + j:cb + C_CONVB + j + 1], ALU.mult, ALU.add, [z[1], b_cst], [acc[1]])
                        P.stt("dve", acc[0][:, 0:TT], z[0][:, 1:513], cst[:, cw + 4 + j:cw + 5 + j], acc[0][:, 0:TT],
                              ALU.mult, ALU.add, [z[1], b_cst, acc[1]], [acc[1]])
                        P.stt("dve", acc[0][:, 0:TT], z[0][:, 0:512], cst[:, cw + j:cw + 1 + j], acc[0][:, 0:TT],
                              ALU.mult, ALU.add, [z[1], b_cst, acc[1]], [acc[1]])
                        P.tt("dve", arena[:, 0 + j, :], pbs[0][:, 0:TT], acc[0][:, 0:TT], ALU.mult, [pbs[1], acc[1]], [b_ar[0 + j]])
                        P.cp("dve", zcarry[:, j, :], z[0][:, 512:514], [z[1]], [b_zc])
                        TF.put(z); TF.put(acc); TF.put(pbs)
                        yield
                    if "B" not in branches:
                        for j in range(4):
                            P.memset("pool", arena[:, 4 + j, :], 0.0, [b_ar[4 + j]])
                    else:
                        slot, rb = ring_get("Bu")
                        sv = slot[:, 0:4096].rearrange("p (k n) -> p k n", k=8)
                        for j in range(4):
                            bk = PS.get()
                            proj_fm(lambda k, j=j: sv[:, k, j * 128:(j + 1) * 128], rb, bk[0], bk[1])
                            P.act(arena[:, 4 + j, :], bk[0][:, :], AF.Gelu_apprx_tanh, [bk[1]], [b_ar[4 + j]])
                            PS.put(bk)
                            yield
                        slot, rb = ring_get("Bv")
                        sv = slot[:, 0:4096].rearrange("p (k n) -> p k n", k=8)
                        for tb in range(4):
                            bk = PS.get()
                            for k in range(8):
                                P.mm(bk[0][:, :], hT[:, k, tb * 128:(tb + 1) * 128], sv[:, k, :], k == 0, k == 7,
                                     [rb, b_h[k]], [bk[1]], k == 7)
                            vg = TF.get()
                            P.act(vg[0][:, 0:TT], bk[0][:, :], AF.Gelu_apprx_tanh, [bk[1]], [vg[1]])
                            PS.put(bk)
                            P.op("dve", lambda e, vg=vg: e.bn_stats(out=small[:, 0:6], in_=vg[0][:, 0:TT]),
                                 reads=[vg[1]], writes=[b_small])
                            P.op("dve", lambda e: e.bn_aggr(out=small[:, 8:10], in_=small[:, 0:6]),
                                 reads=[b_small], writes=[b_small])
                            P.act(small[:, 10:11], small[:, 9:10], AF.Ln, [b_small], [b_small], bias=1e-5)
                            P.act(small[:, 11:12], small[:, 10:11], AF.Exp, [b_small], [b_small], scale=-0.5)
                            P.ts("dve", vg[0][:, 0:TT], vg[0][:, 0:TT], small[:, 8:9], small[:, 11:12], ALU.subtract, ALU.mult,
                                 [vg[1], b_small], [vg[1]])
                            P.tt("pool", vg[0][:, 0:TT], vg[0][:, 0:TT], ltab[:, 0:512], ALU.mult, [vg[1], b_ltab], [vg[1]])
                            P.tt("pool", arena[:, 20 + tb, :], vg[0][:, 0:TT], ltab[:, 512:1024], ALU.add,
                                 [vg[1], b_ltab], [b_ar[20 + tb]])
                            TF.put(vg)
                            yield
                        for g in range(4):
                            bk = PS.get()
                            for tb in range(4):
                                P.mm(bk[0][:, tb * 128:(tb + 1) * 128], arena[:, 20 + tb, g * 128:(g + 1) * 128],
                                     WsT[:, l, g, :], True, True, [b_ar[20 + tb], b_WsT], [bk[1]], tb == 3)
                            tm = TF.get()
                            P.tt("dve", tm[0][:, 0:TT].rearrange("p (a b) -> p a b", a=4),
                                 bk[0][:, :].rearrange("p (a b) -> p a b", a=4),
                                 ltab[:, 1024 + g * 128:1024 + (g + 1) * 128].unsqueeze(1).to_broadcast([128, 4, 128]),
                                 ALU.add, [bk[1], b_ltab], [tm[1]])
                            PS.put(bk)
                            P.tt("dve", arena[:, 4 + g, :], tm[0][:, 0:TT], arena[:, 4 + g, :], ALU.mult,
                                 [tm[1], b_ar[4 + g]], [b_ar[4 + g]])
                            TF.put(tm)
                            yield
                    if "C" in branches:
                        slot, rb = ring_get("Cq")
                        sv = slot[:, 0:4096].rearrange("p (k n) -> p k n", k=8)
                        for h in range(4):
                            bk = PS.get()
                            proj_fm(lambda k, h=h: sv[:, k, h * 128:(h + 1) * 128], rb, bk[0], bk[1])
                            for hf_ in range(2):
                                P.cp("act", qpad[0:64, h, hf_, 0, :], bk[0][0:64, hf_ * 256:(hf_ + 1) * 256], [bk[1]], [b_qp[h]])
                                P.cp("dve", qpad[64:128, h, hf_, 1, :], bk[0][64:128, hf_ * 256:(hf_ + 1) * 256], [bk[1]], [b_qp[h]])
                            PS.put(bk)
                            yield
                        slot, rb = ring_get("Ck")
                        sv = slot[:, 0:4096].rearrange("p (k n) -> p k n", k=8)
                        for h in range(4):
                            bk = PS.get()
                            proj_fm(lambda k, h=h: sv[:, k, h * 128:(h + 1) * 128], rb, bk[0], bk[1])
                            P.cp("dve", KT[:, h, t0:t0 + TT], bk[0][:, :], [bk[1]], [b_KT[h][i]])
                            PS.put(bk)
                            yield
                        slot, rb = ring_get("Cv")
                        sv = slot[:, 0:4096].rearrange("p (k n) -> p k n", k=8)
                        for tb in range(4):
                            bk = PS.get()
                            for k in range(8):
                                P.mm(bk[0][:, :], hT[:, k, tb * 128:(tb + 1) * 128], sv[:, k, :], k == 0, k == 7,
                                     [rb, b_h[k]], [bk[1]], k == 7)
                            P.cp("act", Vc[:, 4 * i + tb, :], bk[0][:, :], [bk[1]], [b_Vc[4 * i + tb]])
                            PS.put(bk)
                            yield

                def attn_units(l=l, i=i, cb=cb, lam_init=lam_init):
                    if "C" not in branches:
                        for j in range(4):
                            P.memset("pool", arena[:, 8 + j, :], 0.0, [b_ar[8 + j]])
                        return
                    pend_e2 = []
                    for h in range(4):
                        for hf in range(2):
                            qlo = 256 * hf
                            kbase = 4 * i + 2 * hf
                            nkt = kbase + 2
                            OB = PS.get(); LB = PS.get()

                            def emit_s(kt, h=h, qlo=qlo, kbase=kbase):
                                j = kt - kbase
                                q0 = 128 * max(j, 0)
                                sb_ = PS.get()
                                kb = b_KT[h][kt // 4]
                                qb = b_qp[h]
                                hf_ = qlo // 256
                                if q0 == 0:
                                    P.mm(sb_[0][:, 0:512], KT[:, h, kt * 128:(kt + 1) * 128],
                                         qpad[:, h, hf_, :, :].rearrange("p a b -> p (a b)"), True, True, [kb, qb], [sb_[1]], True)
                                else:
                                    P.mm(sb_[0][:, q0:256], KT[:, h, kt * 128:(kt + 1) * 128],
                                         qpad[:, h, hf_, 0, q0:256], True, True, [kb, qb], [sb_[1]], True)
                                    P.mm(sb_[0][:, 256 + q0:512], KT[:, h, kt * 128:(kt + 1) * 128],
                                         qpad[:, h, hf_, 1, q0:256], True, True, [kb, qb], [sb_[1]], True)
                                return (sb_, j, q0)

                            sq_ = [emit_s(0), emit_s(1)]
                            for kt in range(nkt):
                                sb_, j, q0 = sq_.pop(0)
                                p = TB.get()
                                if q0 == 0:
                                    P.act(p[0][:, 0:512], sb_[0][:, 0:512], AF.Exp, [sb_[1]], [p[1]], scale=0.125)
                                else:
                                    P.act(p[0][:, q0:256], sb_[0][:, q0:256], AF.Exp, [sb_[1]], [p[1]], scale=0.125)
                                    P.act(p[0][:, 256 + q0:512], sb_[0][:, 256 + q0:512], AF.Exp, [sb_[1]], [p[1]], scale=0.125)
                                PS.put(sb_)
                                if kt == 1 and pend_e2:
                                    pend_e2.pop(0)()
                                if kt + 2 < nkt:
                                    sq_.append(emit_s(kt + 2))
                                if j >= 0:
                                    P.act(p[0][64:128, q0:q0 + 64], p[0][64:128, q0:q0 + 64], AF.Copy, [p[1]], [p[1]], scale=0.0)
                                    P.act(p[0][64:128, 256 + q0:256 + q0 + 64], p[0][64:128, 256 + q0:256 + q0 + 64], AF.Copy,
                                          [p[1]], [p[1]], scale=0.0)
                                first = kt == 0
                                last = kt == nkt - 1
                                if q0 == 0:
                                    rngs = [(0, 512)]
                                else:
                                    rngs = [(q0, 256), (256 + q0, 512)]
                                for ri, (c0, c1) in enumerate(rngs):
                                    lst = last and ri == len(rngs) - 1
                                    P.mm(OB[0][:, c0:c1], Vc[:, kt, h * 128:(h + 1) * 128], p[0][:, c0:c1],
                                         first and ri == 0, lst, [b_Vc[kt], p[1]], [OB[1]], True)
                                    P.mm(LB[0][:, c0:c1], ones_bf[:, :], p[0][:, c0:c1],
                                         first and ri == 0, lst, [b_ones, p[1]], [LB[1]], True)
                                TB.put(p)
                                yield
                            r = TF.get()
                            P.act(r[0][:, 0:TT], LB[0][:, :], AF.Ln, [LB[1]], [r[1]])
                            PS.put(LB)
                            P.act(r[0][:, 0:TT], r[0][:, 0:TT], AF.Exp, [r[1]], [r[1]], scale=-1.0)
                            P.tt("dve", r[0][:, 0:TT], OB[0][:, :], r[0][:, 0:TT], ALU.mult, [OB[1], r[1]], [r[1]])
                            PS.put(OB)
                            o = TF.get()
                            P.stt("dve", o[0][:, 0:256], r[0][:, 256:512], neglam[:, l:l + 1], r[0][:, 0:256], ALU.mult, ALU.add,
                                  [r[1], b_neglam], [o[1]])
                            TF.put(r)
                            sq = TB.get()
                            P.tt("pool", sq[0][:, 0:256], o[0][:, 0:256], o[0][:, 0:256], ALU.mult, [o[1]], [sq[1]])

                            def e2(o=o, sq=sq, h=h, qlo=qlo):
                                bk = PS.get()
                                P.mm(bk[0][:, 0:256], ones_bf[:, :], sq[0][:, 0:256], True, True, [b_ones, sq[1]], [bk[1]], True)
                                TB.put(sq)
                                rs = TF.get()
                                c2 = (1.0 - lam_init) ** 2
                                P.act(rs[0][:, 0:256], bk[0][:, 0:256], AF.Ln, [bk[1]], [rs[1]], bias=1e-5 / c2,
                                      scale=1.0 / (128.0 * c2))
                                PS.put(bk)
                                P.act(rs[0][:, 0:256], rs[0][:, 0:256], AF.Exp, [rs[1]], [rs[1]], scale=-0.5)
                                P.stt("dve", arena[:, 8 + h, qlo:qlo + 256], o[0][:, 0:256],
                                      cst[:, cb + C_SUBLN:cb + C_SUBLN + 1], rs[0][:, 0:256], ALU.mult, ALU.mult,
                                      [o[1], b_cst, rs[1]], [b_ar[8 + h]])
                                TF.put(o); TF.put(rs)
                            pend_e2.append(e2)
                            yield
                    while pend_e2:
                        pend_e2.pop(0)()
                        yield

                def all_units():
                    yield from abc_units()
                    yield from attn_units()

                n_units = (4 if "A" in branches else 0) + (12 if "B" in branches else 0) + \
                          ((12 + 32 * i + 40) if "C" in branches else 0)
                n_pull = max(1, -(-n_units // 35))
                units = all_units()

                if do_D:
                    dst_ = {}
                    dY = {"Y": None}
                    for it in range(32 + 3):
                        g = it - 1
                        if 0 <= g < 32:
                            d = dst_[g]
                            ts_ = st_tab.get()
                            tb_ = st_tab.bufs[ts_]
                            a = TF.get(); b = TF.get()
                            P.tt("dve", a[0][:, 0:TT], d["S"][0][:, :], sr_tab[:, ts_, 0:512], ALU.mult, [d["S"][1], tb_], [a[1]])
                            P.tt("dve", b[0][:, 0:TT], d["Sw"][0][:, :], sr_tab[:, ts_, 512:1024], ALU.mult, [d["Sw"][1], tb_], [b[1]])
                            PS.put(d["S"]); PS.put(d["Sw"])
                            P.tt("pool", a[0][:, 0:TT], a[0][:, 0:TT], b[0][:, 0:TT], ALU.add, [a[1], b[1]], [a[1]])
                            TF.put(b)
                            d["a"] = a; d["ts"] = ts_
                        g = it - 2
                        if 0 <= g < 32:
                            d = dst_[g]
                            a = d["a"]; ts_ = d["ts"]; tb_ = st_tab.bufs[ts_]
                            Z = TF.get()
                            P.op("dve", lambda e, a=a, Z=Z, g=g, l=l: e.tensor_tensor_scan(
                                out=Z[0][:, 0:TT], data0=RHO[:, l, g:g + 1].to_broadcast([128, TT]), data1=a[0][:, 0:TT],
                                initial=scarry[:, g:g + 1], op0=ALU.mult, op1=ALU.add),
                                reads=[a[1], b_RHO, b_scarry], writes=[Z[1]])
                            TF.put(a)
                            zc = TB.get(); zs = TB.get()
                            P.tt("pool", zc[0], Z[0][:, 0:TT], sr_tab[:, ts_, 0:512], ALU.mult, [Z[1], tb_], [zc[1]])
                            P.tt("dve", zs[0], Z[0][:, 0:TT], sr_tab[:, ts_, 512:1024], ALU.mult, [Z[1], tb_], [zs[1]])
                            P.cp("pool", zl[:, g:g + 1], Z[0][:, TT - 1:TT], [Z[1]], [b_zl])
                            TF.put(Z)
                            d["zc"] = zc; d["zs"] = zs
                        g = it - 3
                        if 0 <= g < 32:
                            d = dst_.pop(g)
                            j = g // 8
                            sc_ = st_cc.get()
                            cbuf = st_cc.bufs[sc_]
                            if g % 8 == 0:
                                dY["Y"] = PS.get()
                            Y = dY["Y"]
                            P.mm(Y[0][:, :], sr_cc[:, sc_, 0:128], d["zc"][0], g % 8 == 0, False, [cbuf, d["zc"][1]], [Y[1]], True)
                            P.mm(Y[0][:, :], sr_cc[:, sc_, 128:256], d["zs"][0], False, False, [cbuf, d["zs"][1]], [Y[1]], True)
                            TB.put(d["zc"]); TB.put(d["zs"])
                            if g % 8 == 7:
                                P.mm(Y[0][:, :], Kdir[:, l, j, :], arena[:, 16 + j, :], False, True,
                                     [b_Kdir, b_ar[16 + j]], [Y[1]], True)
                                P.act(arena[:, 24 + j, :], Y[0][:, :], AF.Gelu_apprx_tanh, [Y[1]], [b_ar[24 + j]])
                                PS.put(Y)
                        g = it
                        if g < 32:
                            j = g // 8
                            sb = st_bb.get()
                            S = PS.get(); Sw = PS.get()
                            P.mm(S[0][:, :], sr_bb[:, sb, 0:128], arena[:, 16 + j, :], True, True,
                                 [st_bb.bufs[sb], b_ar[16 + j]], [S[1]], True)
                            P.mm(Sw[0][:, :], sr_bb[:, sb, 128:256], arena[:, 16 + j, :], True, True,
                                 [st_bb.bufs[sb], b_ar[16 + j]], [Sw[1]], True)
                            dst_[g] = {"S": S, "Sw": Sw}
                        for _ in range(n_pull):
                            next(units, None)
                    P.tt("dve", ccb[:, 0:32], zl[:, :], COSL[:, l, :], ALU.mult, [b_zl, b_COSL], [b_ccb])
                    P.tt("dve", ccb[:, 32:64], zl[:, :], SINL[:, l, :], ALU.mult, [b_zl, b_COSL], [b_ccb])
                    cbk = PS.get()
                    P.mm(cbk[0][:, 0:32], cst[:, C_IDENT:C_IDENT + 128], ccb[:, 0:32], True, False, [b_cst, b_ccb], [cbk[1]], True)
                    P.mm(cbk[0][:, 0:32], cst[:, C_JMAT:C_JMAT + 128], ccb[:, 32:64], False, True, [b_cst, b_ccb], [cbk[1]], True)
                    P.cp("act", scarry[:, :], cbk[0][:, 0:32], [cbk[1]], [b_scarry])
                    PS.put(cbk)
                for _ in units:
                    pass

                if do_D:
                    slot, rb = ring_get("Wglu")
                    sv = slot[:, 0:2048].rearrange("p (c n) -> p c n", c=4)
                    for j in range(4):
                        bk = PS.get()
                        for c in range(4):
                            P.mm(bk[0][:, :], sv[:, c, j * 128:(j + 1) * 128], arena[:, 24 + c, :], c == 0, c == 3,
                                 [rb, b_ar[24 + c]], [bk[1]], c == 3)
                        sg = TF.get()
                        P.act(sg[0][:, 0:TT], bk[0][:, :], AF.Sigmoid, [bk[1], b_cst], [sg[1]],
                              bias=cst[:, cb + C_BGLU + j:cb + C_BGLU + j + 1])
                        PS.put(bk)
                        P.tt("dve", arena[:, 12 + j, :], arena[:, 24 + j, :], sg[0][:, 0:TT], ALU.mult,
                             [b_ar[24 + j], sg[1]], [b_ar[12 + j]])
                        TF.put(sg)

                for m in range(8):
                    gslot, grb = ring_get("G%d" % m)
                    gv = gslot[:, 0:4096].rearrange("p (k b n) -> p k b n", k=8, b=4)
                    bslot, brb = ring_get("BR%d" % m)
                    bv = bslot[:, 0:2048].rearrange("p (b c n) -> p b c n", b=4, c=4)
                    prods = []
                    for b in range(4):
                        gk = PS.get()
                        proj_fm(lambda k, b=b: gv[:, k, b, :], grb, gk[0], gk[1])
                        gt = TF.get()
                        P.act(gt[0][:, 0:TT], gk[0][:, :], AF.Sigmoid, [gk[1]], [gt[1]])
                        PS.put(gk)
                        rk = PS.get()
                        for c in range(4):
                            P.mm(rk[0][:, :], bv[:, b, c, :], arena[:, 4 * b + c, :], c == 0, c == 3,
                                 [brb, b_ar[4 * b + c]], [rk[1]], c == 3)
                        P.tt("dve", gt[0][:, 0:TT], rk[0][:, :], gt[0][:, 0:TT], ALU.mult, [rk[1], gt[1]], [gt[1]])
                        PS.put(rk)
                        prods.append(gt)
                    P.tt("pool", prods[0][0][:, 0:TT], prods[0][0][:, 0:TT], prods[1][0][:, 0:TT], ALU.add,
                         [prods[0][1], prods[1][1]], [prods[0][1]])
                    P.tt("pool", prods[2][0][:, 0:TT], prods[2][0][:, 0:TT], prods[3][0][:, 0:TT], ALU.add,
                         [prods[2][1], prods[3][1]], [prods[2][1]])
                    P.tt("pool", arena[:, 16 + m, :], prods[0][0][:, 0:TT], prods[2][0][:, 0:TT], ALU.add,
                         [prods[0][1], prods[2][1]], [b_ar[16 + m]])
                    for pr in prods:
                        TF.put(pr)
                for half in range(2):
                    slot, rb = ring_get("WO%d" % half)
                    sv = slot[:, 0:4096].rearrange("p (k n) -> p k n", k=8)
                    for mm_ in range(4):
                        m = half * 4 + mm_
                        bk = PS.get()
                        for k in range(8):
                            P.mm(bk[0][:, :], sv[:, k, mm_ * 128:(mm_ + 1) * 128], arena[:, 16 + k, :], k == 0, k == 7,
                                 [rb, b_ar[16 + k]], [bk[1]], k == 7)
                        P.tt("dve", x[:, m, :], x[:, m, :], bk[0][:, :], ALU.add, [b_x[m], bk[1]], [b_x[m]])
                        PS.put(bk)

                if do_ffn:
                    norm_to_h(cb + C_GFFN)
                    for fp in range(11):
                        slot, rb = ring_get("F%d" % fp)
                        sv = slot[:, 0:4096].rearrange("p (f s k n) -> p f s k n", f=2, s=2, k=8)
                        for fi in range(2):
                            f = 2 * fp + fi
                            gk = PS.get(); uk = PS.get()
                            proj_fm(lambda k, fi=fi: sv[:, fi, 0, k, :], rb, gk[0], gk[1])
                            proj_fm(lambda k, fi=fi: sv[:, fi, 1, k, :], rb, uk[0], uk[1])
                            sg = TF.get()
                            P.act(sg[0][:, 0:TT], gk[0][:, :], AF.Silu, [gk[1]], [sg[1]])
                            PS.put(gk)
                            P.tt("dve", arena[:, f, :], uk[0][:, :], sg[0][:, 0:TT], ALU.mult, [uk[1], sg[1]], [b_ar[f]])
                            PS.put(uk); TF.put(sg)
                    for m in range(8):
                        slot, rb = ring_get("D%d" % m)
                        sv = slot[:, 0:2816].rearrange("p (f n) -> p f n", f=22)
                        bk = PS.get()
                        for f in range(22):
                            P.mm(bk[0][:, :], sv[:, f, :], arena[:, f, :], f == 0, f == 21, [rb, b_ar[f]], [bk[1]], f == 21)
                        P.tt("dve", x[:, m, :], x[:, m, :], bk[0][:, :], ALU.add, [b_x[m], bk[1]], [b_x[m]])
                        PS.put(bk)
                        if l < NL - 1:
                            P.dma("pool", xs.rearrange("(c p) t -> p c t", p=128)[:, m, t0:t0 + TT], x[:, m, :],
                                  [b_x[m]], [], xst_c[m])

                if l == NL - 1:
                    rt = rmsnorm_stats(1e-6, 1024.0)
                    dstv = outT.rearrange("(c p) t -> p c t", p=128)
                    for c in range(8):
                        P.stt("dve", x[:, c, :], x[:, c, :],
                              cst[:, C_GFINAL + c:C_GFINAL + c + 1], rt[0][:, 0:TT],
                              ALU.mult, ALU.mult, [b_x[c], b_cst, rt[1]], [b_x[c]])
                        P.dma("pool", dstv[:, c, t0:t0 + TT], x[:, c, :], [b_x[c]], [], xst_c[c])
                    TF.put(rt)
                elif not do_ffn:
                    dstv = xs.rearrange("(c p) t -> p c t", p=128)
                    for c in range(8):
                        P.dma("pool", dstv[:, c, t0:t0 + TT], x[:, c, :], [b_x[c]], [], xst_c[c])
        P.wait_all("sp", [(d_.sem, d_.cnt) for d_ in xst_c])
        P.replay()
    return nc


def _ssm_prologue(nc, P, pst, sbuf, NL, ssmraw_in, praw, b_praw, cst, b_cst, ssw, sst, ssm_sem, b_ssm,
                  Kdir, b_Kdir, RHO, b_RHO, PS, ld_sem):
    NWK = 16
    wk = sbuf("ssm_wk", [128, NWK, 512], F32, pst)
    b_wk = [Buf() for _ in range(NWK)]
    bbN = sbuf("bbN", [128, 2, 512], F32, pst); b_bbN = Buf()
    ccN = sbuf("ccN", [128, 2, 512], F32, pst); b_ccN = Buf()
    bbT = sbuf("bbT", [128, 2, 512], F32, pst); b_bbT = Buf()
    wst_l = [(sbuf("wst%d" % k, [128, 8, 512], BF16, pst), Buf()) for k in range(2)]
    Ct_l = [(sbuf("Ctab%d" % k, [128, 8, 512], F32, pst), Buf()) for k in range(2)]
    St_l = [(sbuf("Stab%d" % k, [128, 8, 512], F32, pst), Buf()) for k in range(2)]
    tm1 = sbuf("tm1", [128, 8, 256], F32, pst); b_tm1 = Buf()
    tm2 = sbuf("tm2", [128, 8, 256], F32, pst); b_tm2 = Buf()

    def W(i, n):
        return wk[:, i, 0:n]

    def coef(n, are, aim, ldt, keep_trig):
        b = b_wk
        DT, T, TH, SN, CS, RH, LRE, LIM, NR, DEN, CRE, CIM, T2 = range(13)
        rd = [b_raw]
        P.act(W(DT, n), ldt, AF.Exp, rd, [b[DT]])
        P.tt("dve", W(T, n), W(DT, n), are, ALU.mult, [b[DT]] + rd, [b[T]])
        P.act(W(RH, n), W(T, n), AF.Exp, [b[T]], [b[RH]])
        P.tt("dve", W(TH, n), W(DT, n), aim, ALU.mult, [b[DT]] + rd, [b[TH]])
        P.ts("dve", W(T, n), W(TH, n), 1.0 / (2 * PI), MAGIC, ALU.mult, ALU.add, [b[TH]], [b[T]])
        P.ts("dve", W(T, n), W(T, n), MAGIC, -2 * PI, ALU.subtract, ALU.mult, [b[T]], [b[T]])
        P.tt("dve", W(TH, n), W(TH, n), W(T, n), ALU.add, [b[TH], b[T]], [b[TH]])
        P.ts("dve", W(TH, n), W(TH, n), -3.14159, 3.14159, ALU.max, ALU.min, [b[TH]], [b[TH]])
        P.act(W(SN, n), W(TH, n), AF.Sin, [b[TH]], [b[SN]])
        P.ts("dve", W(T2, n), W(TH, n), PI / 2, None, ALU.add, None, [b[TH]], [b[T2]])
        P.ts("dve", W(T, n), W(T2, n), PI, -2 * PI, ALU.is_gt, ALU.mult, [b[T2]], [b[T]])
        P.tt("dve", W(T2, n), W(T2, n), W(T, n), ALU.add, [b[T2], b[T]], [b[T2]])
        P.ts("dve", W(T2, n), W(T2, n), -3.14159, 3.14159, ALU.max, ALU.min, [b[T2]], [b[T2]])
        P.act(W(CS, n), W(T2, n), AF.Sin, [b[T2]], [b[CS]])
        P.tt("dve", W(LRE, n), W(RH, n), W(CS, n), ALU.mult, [b[RH], b[CS]], [b[LRE]])
        P.tt("dve", W(LIM, n), W(RH, n), W(SN, n), ALU.mult, [b[RH], b[SN]], [b[LIM]])
        P.ts("dve", W(NR, n), W(LRE, n), -1.0, None, ALU.add, None, [b[LRE]], [b[NR]])
        P.tt("dve", W(DEN, n), are, are, ALU.mult, rd, [b[DEN]])
        P.tt("dve", W(T, n), aim, aim, ALU.mult, rd, [b[T]])
        P.tt("dve", W(DEN, n), W(DEN, n), W(T, n), ALU.add, [b[DEN], b[T]], [b[DEN]])
        P.op("dve", lambda e: e.reciprocal(out=W(DEN, n), in_=W(DEN, n)), reads=[b[DEN]], writes=[b[DEN]])
        P.tt("dve", W(CRE, n), W(NR, n), are, ALU.mult, [b[NR]] + rd, [b[CRE]])
        P.tt("dve", W(T, n), W(LIM, n), aim, ALU.mult, [b[LIM]] + rd, [b[T]])
        P.tt("dve", W(CRE, n), W(CRE, n), W(T, n), ALU.add, [b[CRE], b[T]], [b[CRE]])
        P.tt("dve", W(CRE, n), W(CRE, n), W(DEN, n), ALU.mult, [b[CRE], b[DEN]], [b[CRE]])
        P.tt("dve", W(CIM, n), W(LIM, n), are, ALU.mult, [b[LIM]] + rd, [b[CIM]])
        P.tt("dve", W(T, n), W(NR, n), aim, ALU.mult, [b[NR]] + rd, [b[T]])
        P.tt("dve", W(CIM, n), W(CIM, n), W(T, n), ALU.subtract, [b[CIM], b[T]], [b[CIM]])
        P.tt("dve", W(CIM, n), W(CIM, n), W(DEN, n), ALU.mult, [b[CIM], b[DEN]], [b[CIM]])
        return CRE, CIM, RH, CS, SN

    for l in range(NL):
        raw = ld_sem[3][:, l, :]
        b_raw = ld_sem[4][l]
        CRE, CIM, RH, CS, SN = coef(32, raw[:, SN_ARE:SN_ARE + 32], raw[:, SN_AIM:SN_AIM + 32],
                                    raw[:, SN_LDT:SN_LDT + 32], True)
        P.cp("dve", RHO[:, l, :], W(RH, 32), [b_wk[RH]], [b_RHO])
        def bc(i):
            return wk[:, i, 0:32].unsqueeze(2).to_broadcast([128, 32, 16])
        bre = raw[:, SN_BRE:SN_BRE + 512].rearrange("p (g h) -> p g h", g=32)
        bim = raw[:, SN_BIM:SN_BIM + 512].rearrange("p (g h) -> p g h", g=32)
        A1, A2 = 13, 14
        v3 = lambda ap: ap.rearrange("p (g h) -> p g h", g=32)
        P.tt("dve", v3(W(A1, 512)), bre, bc(CRE), ALU.mult, [b_raw, b_wk[CRE]], [b_wk[A1]])
        P.tt("dve", v3(W(15, 512)), bim, bc(CIM), ALU.mult, [b_raw, b_wk[CIM]], [b_wk[15]])
        P.tt("dve", W(A1, 512), W(A1, 512), W(15, 512), ALU.subtract, [b_wk[A1], b_wk[15]], [b_wk[A1]])
        P.tt("dve", v3(W(A2, 512)), bim, bc(CRE), ALU.mult, [b_raw, b_wk[CRE]], [b_wk[A2]])
        P.tt("dve", v3(W(15, 512)), bre, bc(CIM), ALU.mult, [b_raw, b_wk[CIM]], [b_wk[15]])
        P.tt("dve", W(A2, 512), W(A2, 512), W(15, 512), ALU.add, [b_wk[A2], b_wk[15]], [b_wk[A2]])
        P.cp("dve", bbN[0:64, 0, :], wk[0:64, A1, :], [b_wk[A1]], [b_bbN])
        P.cp("dve", bbN[64:128, 0, :], wk[64:128, A2, :], [b_wk[A2]], [b_bbN])
        cre_ = raw[:, SN_CRE:SN_CRE + 512]
        cim_ = raw[:, SN_CIM:SN_CIM + 512]
        P.cp("dve", ccN[0:64, 0, :], cre_[0:64, :], [b_raw], [b_ccN])
        P.ts("dve", ccN[64:128, 0, :], cim_[64:128, :], -1.0, None, ALU.mult, None, [b_raw], [b_ccN])
        P.ts("dve", ccN[0:64, 1, :], cim_[0:64, :], -1.0, None, ALU.mult, None, [b_raw], [b_ccN])
        P.ts("dve", ccN[64:128, 1, :], cre_[64:128, :], -1.0, None, ALU.mult, None, [b_raw], [b_ccN])
        for j in range(4):
            dcol = cst[:, l * CL + C_SSMD + j:l * CL + C_SSMD + j + 1]
            P.ts("dve", Kdir[:, l, j, :], cst[:, C_IDENT:C_IDENT + 128], dcol, None, ALU.mult, None, [b_cst], [b_Kdir])
        for gc in range(4):
            gs = slice(8 * gc, 8 * gc + 8)
            par = gc % 2
            Ct, b_Ct = Ct_l[par]
            St, b_St = St_l[par]
            P.cp("dve", Ct[:, :, 0:1], wk[:, CS, gs].unsqueeze(2), [b_wk[CS]], [b_Ct])
            P.cp("dve", St[:, :, 0:1], wk[:, SN, gs].unsqueeze(2), [b_wk[SN]], [b_St])
            m = 1
            while m < 512:
                cm = Ct[:, :, m - 1:m].to_broadcast([128, 8, m])
                sm = St[:, :, m - 1:m].to_broadcast([128, 8, m])
                P.tt("dve", tm1[:, :, 0:m], St[:, :, 0:m], sm, ALU.mult, [b_St], [b_tm1])
                P.tt("dve", tm2[:, :, 0:m], Ct[:, :, 0:m], sm, ALU.mult, [b_Ct, b_St], [b_tm2])
                P.tt("dve", Ct[:, :, m:2 * m], Ct[:, :, 0:m], cm, ALU.mult, [b_Ct], [b_Ct])
                P.tt("dve", St[:, :, m:2 * m], St[:, :, 0:m], cm, ALU.mult, [b_St, b_Ct], [b_St])
                P.tt("dve", Ct[:, :, m:2 * m], Ct[:, :, m:2 * m], tm1[:, :, 0:m], ALU.subtract, [b_Ct, b_tm1], [b_Ct])
                P.tt("dve", St[:, :, m:2 * m], St[:, :, m:2 * m], tm2[:, :, 0:m], ALU.add, [b_St, b_tm2], [b_St])
                m *= 2
            COSL, SINL, b_COSL = ld_sem[0:3]
            P.cp("dve", COSL[:, l, gs].unsqueeze(2), Ct[:, :, 511:512], [b_Ct], [b_COSL])
            P.cp("dve", SINL[:, l, gs].unsqueeze(2), St[:, :, 511:512], [b_St], [b_COSL])
            P.dma("sp", sst[l, 8 * gc:8 * gc + 8, :, 0:512].rearrange("g q k -> q g k"), Ct[:, :, :],
                  [b_Ct], [b_ssm[l][0 + par]], ssm_sem[0] if par == 0 else ssm_sem[4])
            P.dma("sp", sst[l, 8 * gc:8 * gc + 8, :, 512:1024].rearrange("g q k -> q g k"), St[:, :, :],
                  [b_St], [b_ssm[l][2 + par]], ssm_sem[1] if par == 0 else ssm_sem[5])
        CRE, CIM, RH, CS, SN = coef(512, raw[:, ST_ARE:ST_ARE + 512], raw[:, ST_AIM:ST_AIM + 512],
                                    raw[:, ST_LDT:ST_LDT + 512], False)
        btre = raw[:, ST_BRE:ST_BRE + 512]
        btim = raw[:, ST_BIM:ST_BIM + 512]
        P.tt("dve", W(A1, 512), btre, W(CRE, 512), ALU.mult, [b_raw, b_wk[CRE]], [b_wk[A1]])
        P.tt("dve", W(15, 512), btim, W(CIM, 512), ALU.mult, [b_raw, b_wk[CIM]], [b_wk[15]])
        P.tt("dve", W(A1, 512), W(A1, 512), W(15, 512), ALU.subtract, [b_wk[A1], b_wk[15]], [b_wk[A1]])
        P.tt("dve", W(A2, 512), btim, W(CRE, 512), ALU.mult, [b_raw, b_wk[CRE]], [b_wk[A2]])
        P.tt("dve", W(15, 512), btre, W(CIM, 512), ALU.mult, [b_raw, b_wk[CIM]], [b_wk[15]])
        P.tt("dve", W(A2, 512), W(A2, 512), W(15, 512), ALU.add, [b_wk[A2], b_wk[15]], [b_wk[A2]])
        v4 = lambda ap: ap.rearrange("p (j q) -> p j q", j=4)
        P.cp("dve", v4(bbT[:, 0, :])[:, :, 0:64], v4(W(A1, 512))[:, :, 0:64], [b_wk[A1]], [b_bbT])
        P.cp("dve", v4(bbT[:, 0, :])[:, :, 64:128], v4(W(A2, 512))[:, :, 64:128], [b_wk[A2]], [b_bbT])
        P.cp("dve", v4(bbT[:, 1, :])[:, :, 0:64], v4(W(A2, 512))[:, :, 0:64], [b_wk[A2]], [b_bbT])
        P.ts("dve", v4(bbT[:, 1, :])[:, :, 64:128], v4(W(A1, 512))[:, :, 64:128], -1.0, None, ALU.mult, None,
             [b_wk[A1]], [b_bbT])
        for j in range(4):
            par = j % 2
            wst, b_wst = wst_l[par]
            P.memset("dve", wst[:], 0.0, [b_wst])
            for gp in range(8):
                g = 8 * j + gp
                mcol = praw[:, PR_M8 + gp:PR_M8 + gp + 1]
                P.ts("dve", wst[:, gp, 0:128], bbT[:, 0, j * 128:(j + 1) * 128], mcol, None, ALU.mult, None,
                     [b_bbT, b_praw], [b_wst])
                P.ts("dve", wst[:, gp, 128:256], bbT[:, 1, j * 128:(j + 1) * 128], mcol, None, ALU.mult, None,
                     [b_bbT, b_praw], [b_wst])
                P.cp("dve", wst[:, gp, 256 + 16 * gp:256 + 16 * gp + 16], ccN[:, 0, g * 16:(g + 1) * 16], [b_ccN], [b_wst])
                P.cp("dve", wst[:, gp, 384 + 16 * gp:384 + 16 * gp + 16], ccN[:, 1, g * 16:(g + 1) * 16], [b_ccN], [b_wst])
            P.dma("sp", ssw[l, 8 * j:8 * j + 8, :, :].rearrange("g q n -> q g n"), wst[:, :, :],
                  [b_wst], [b_ssm[l][4 + par]], ssm_sem[2] if par == 0 else ssm_sem[6])


def _km(cols):
    return cols.reshape(8, 128, -1).transpose(1, 0, 2)


def pack_weights(inp):
    out = np.empty((2, 128, NW), np.float32)
    for l in range(2):
        w_in = np.asarray(inp["w_in"][l], np.float32)
        parts = []
        A = w_in[:, 0:1536]
        for j in range(4):
            blk = np.stack([_km(A[:, s * 512 + j * 128:s * 512 + (j + 1) * 128]) for s in range(3)], axis=2)
            parts.append(blk.reshape(128, -1))
        for c0 in (1536, 2048, 2560, 3072, 3584):
            parts.append(_km(w_in[:, c0:c0 + 512]).reshape(128, -1))
        parts.insert(0, _km(w_in[:, 4096:4608]).reshape(128, -1))
        parts.append(np.asarray(inp["w_glu"][l]).reshape(4, 128, 512).transpose(1, 0, 2).reshape(128, -1))
        G = w_in[:, 4608:8704]
        w_br = np.asarray(inp["w_br"][l])
        for m in range(8):
            gm = np.stack([_km(G[:, b * 1024 + m * 128:b * 1024 + (m + 1) * 128]) for b in range(4)], axis=2)
            parts.append(gm.reshape(128, -1))
            br = w_br[:, :, m * 128:(m + 1) * 128].reshape(4, 4, 128, 128).transpose(2, 0, 1, 3)
            parts.append(br.reshape(128, -1))
        wo = np.asarray(inp["w_o"][l])
        for half in range(2):
            parts.append(_km(wo[:, half * 512:(half + 1) * 512]).reshape(128, -1))
        wg = np.asarray(inp["w_ffn_gate"][l])
        wu = np.asarray(inp["w_ffn_up"][l])
        for fp in range(11):
            blk = np.stack([np.stack([_km(wg[:, f * 128:(f + 1) * 128]), _km(wu[:, f * 128:(f + 1) * 128])], axis=1)
                            for f in (2 * fp, 2 * fp + 1)], axis=1)
            parts.append(blk.reshape(128, -1))
        wd = np.asarray(inp["w_ffn_down"][l])
        for m in range(8):
            parts.append(wd[:, m * 128:(m + 1) * 128].reshape(22, 128, 128).transpose(1, 0, 2).reshape(128, -1))
        cat = np.concatenate(parts, axis=1)
        assert cat.shape == (128, NW), cat.shape
        out[l] = cat
    return out


def pack_small(inp):
    f = lambda a: np.asarray(a, np.float32)
    cst = np.zeros((128, NCST), np.float32)
    ltab = np.zeros((2, 128, 1536), np.float32)
    praw = np.zeros((128, NPR), np.float32)
    ssmraw = np.zeros((2, 128, NSR), np.float32)
    for l in range(2):
        b = l * CL
        cst[:, b + C_GMIX:b + C_GMIX + 8] = f(inp["g_mix"][l]).reshape(8, 128).T
        cst[:, b + C_GFFN:b + C_GFFN + 8] = f(inp["g_ffn"][l]).reshape(8, 128).T
        cw = f(inp["conv_w"][l])
        for k in range(3):
            cst[:, b + C_CONVW + 4 * k:b + C_CONVW + 4 * k + 4] = cw[k].reshape(4, 128).T
        cst[:, b + C_CONVB:b + C_CONVB + 4] = f(inp["conv_b"][l]).reshape(4, 128).T
        cst[:, b + C_BGLU:b + C_BGLU + 4] = f(inp["b_glu"][l]).reshape(4, 128).T
        cst[:, b + C_SSMD:b + C_SSMD + 4] = f(inp["ssm_d"][l]).reshape(4, 128).T
        cst[:, b + C_SUBLN] = f(inp["subln_g"][l])
        ltab[l, :, 0:512] = f(inp["sg_ln_g"][l])[None, :]
        ltab[l, :, 512:1024] = f(inp["sg_ln_b"][l])[None, :]
        ltab[l, :, 1024:1536] = f(inp["sg_b"][l]).reshape(1, 512)
        praw[:, l * PR_L:l * PR_L + 512] = f(inp["sg_w"][l]).transpose(2, 0, 1).reshape(128, 512)
        praw[:, l * PR_L + 512:l * PR_L + 768] = f(inp["lam_qk"][l]).reshape(1, 256)
        a_re = f(inp["ssm_a_re"][l]); a_im = f(inp["ssm_a_im"][l]); ldt = f(inp["ssm_log_dt"][l])
        b_re = f(inp["ssm_b_re"][l]); b_im = f(inp["ssm_b_im"][l])
        c_re = f(inp["ssm_c_re"][l]); c_im = f(inp["ssm_c_im"][l])
        sr = ssmraw[l]
        sr[:, SN_ARE:SN_ARE + 32] = np.tile(a_re.T, (2, 1))
        sr[:, SN_AIM:SN_AIM + 32] = np.tile(a_im.T, (2, 1))
        sr[:, SN_LDT:SN_LDT + 32] = ldt[None, :]
        sr[:, SN_BRE:SN_BRE + 512] = np.tile(b_re.transpose(1, 0, 2), (2, 1, 1)).reshape(128, 512)
        sr[:, SN_BIM:SN_BIM + 512] = np.tile(b_im.transpose(1, 0, 2), (2, 1, 1)).reshape(128, 512)
        sr[:, SN_CRE:SN_CRE + 512] = np.tile(c_re.transpose(2, 0, 1), (2, 1, 1)).reshape(128, 512)
        sr[:, SN_CIM:SN_CIM + 512] = np.tile(c_im.transpose(2, 0, 1), (2, 1, 1)).reshape(128, 512)

        def tl(a):
            t = np.tile(a.reshape(4, 8, 64), (1, 1, 2)).transpose(1, 0, 2)
            return np.repeat(t[:, None], 16, axis=1).reshape(128, 512)
        sr[:, ST_ARE:ST_ARE + 512] = tl(a_re)
        sr[:, ST_AIM:ST_AIM + 512] = tl(a_im)
        sr[:, ST_LDT:ST_LDT + 512] = tl(np.repeat(ldt[:, None], 64, axis=1))

        def tb(bb):
            t = bb.reshape(4, 8, 64, 16).transpose(1, 3, 0, 2)
            return np.tile(t, (1, 1, 1, 2)).reshape(128, 512)
        sr[:, ST_BRE:ST_BRE + 512] = tb(b_re)
        sr[:, ST_BIM:ST_BIM + 512] = tb(b_im)
    cst[:, C_GFINAL:C_GFINAL + 8] = f(inp["g_final"]).reshape(8, 128).T
    cst[:, C_IDENT:C_IDENT + 128] = np.eye(128, dtype=np.float32)
    J = np.zeros((128, 128), np.float32)
    for q in range(64):
        J[q + 64, q] = -1.0
        J[q, q + 64] = 1.0
    cst[:, C_JMAT:C_JMAT + 128] = J
    s_ = np.arange(128)
    praw[:, PR_TRIL:PR_TRIL + 128] = (s_[:, None] <= s_[None, :]).astype(np.float32)
    praw[:, PR_BD:PR_BD + 128] = (s_[:, None] // 16 == s_[None, :] // 16).astype(np.float32)
    praw[:, PR_M8:PR_M8 + 8] = (s_[:, None] // 16 == np.arange(8)[None, :]).astype(np.float32)
    return cst, ltab, praw, ssmraw


_CACHE = {}


def kernel(**inputs):
    x = np.asarray(inputs["x"], np.float32)
    wpk = pack_weights(inputs)
    cst, ltab, praw, ssmraw = pack_small(inputs)
    if "nc" not in _CACHE:
        _CACHE["nc"] = build_program()
    nc = _CACHE["nc"]
    in_maps = []
    for b in range(8):
        in_maps.append({"xT": np.ascontiguousarray(x[b].T), "wpk": wpk, "cst": cst, "ltab": ltab,
                        "praw": praw, "ssmraw": ssmraw})
    res = run_bass_kernel_spmd(nc, in_maps, core_ids=list(range(8)))
    out = np.stack([np.ascontiguousarray(res.results[b]["outT"].T) for b in range(8)], axis=0)
    return out.astype(np.float32)
```
